# Optimizing a Trainium2 kernel written in Bass

```python
import math
import jax
import jax.numpy as jnp
from jax import lax
import numpy as np

D_MODEL = 1024
BATCH = 32
SEQ = 2048
DEPTH = 4

CHUNK = 64
Q_BLOCK = 128
HEAD_DIM = 64
ROPE_DIM = HEAD_DIM // 4
ROPE_THETA = 500000.0
RMS_EPS = 1e-6
A_HEADS = D_MODEL // (2 * HEAD_DIM)
A_WIDTH = A_HEADS * HEAD_DIM
B_HEADS = D_MODEL // (4 * HEAD_DIM)
B_QK_WIDTH = B_HEADS * 2 * HEAD_DIM
B_VDIM = 2 * HEAD_DIM
B_WIDTH = B_HEADS * B_VDIM
C_WIDTH = D_MODEL // 2
CONV_W = 3
D_HEADS = D_MODEL // (2 * HEAD_DIM)
D_WIDTH = D_HEADS * HEAD_DIM
D_LEFT_CHUNKS = 8
D_BAND = (D_LEFT_CHUNKS + 1) * CHUNK
REL_CLIP = 128
X_HEADS = 4
X_HEAD_DIM = D_MODEL // X_HEADS
N_MEM = 256
D_FF = 2816
N_EVEN = (DEPTH + 1) // 2
N_ODD = DEPTH // 2
EVEN_SIZES = (A_WIDTH, A_WIDTH, A_WIDTH, A_HEADS, B_QK_WIDTH, B_QK_WIDTH, B_WIDTH)
ODD_SIZES = (C_WIDTH, C_WIDTH, C_WIDTH, D_WIDTH, D_WIDTH, D_WIDTH)
EVEN_IN = 3 * A_WIDTH + A_HEADS + 2 * B_QK_WIDTH + B_WIDTH
ODD_IN = 3 * C_WIDTH + 3 * D_WIDTH
EVEN_MIX = A_WIDTH + B_WIDTH
ODD_MIX = C_WIDTH + D_WIDTH
MAX_POS_OFFSET = 65536

kernel_name = 'hybrid_fox_diff_conv_chunkattn_macaron'


def rms_norm(x, g):
    xf = x.astype(jnp.float32)
    y = xf * lax.rsqrt(jnp.mean(xf * xf, axis=-1, keepdims=True) + RMS_EPS)
    return (y * g.astype(jnp.float32)).astype(x.dtype)


def split_cols(y, sizes):
    out, start = [], 0
    for s in sizes:
        out.append(y[..., start:start + s])
        start += s
    return out


def rope_tables(positions):
    inv = ROPE_THETA ** (-jnp.arange(0, ROPE_DIM, 2, dtype=jnp.float32) / ROPE_DIM)
    ang = positions.astype(jnp.float32)[..., None] * inv
    return jnp.cos(ang), jnp.sin(ang)


def apply_partial_rope(x, cos, sin):
    half = ROPE_DIM // 2
    bshape = cos.shape[:2] + (1,) * (x.ndim - 3) + (half,)
    c, s = cos.reshape(bshape), sin.reshape(bshape)
    xr = x[..., :ROPE_DIM].astype(jnp.float32)
    x1, x2 = xr[..., :half], xr[..., half:]
    rot = jnp.concatenate([x1 * c - x2 * s, x2 * c + x1 * s], axis=-1).astype(x.dtype)
    return jnp.concatenate([rot, x[..., ROPE_DIM:]], axis=-1)


def swiglu(h, w_in, w_out):
    g, u = split_cols(h @ w_in, (D_FF, D_FF))
    return (jax.nn.silu(g) * u) @ w_out


def forgetting_attention(q, k, v, cum_logf):
    seq = q.shape[1]
    scale = q.shape[-1] ** -0.5
    outs = []
    for qs in range(0, seq, Q_BLOCK):
        qe = qs + Q_BLOCK
        s = jnp.einsum('bqhd,bkhd->bhqk', q[:, qs:qe], k[:, :qe],
                       preferred_element_type=jnp.float32) * scale
        s = s + cum_logf[:, :, qs:qe, None] - cum_logf[:, :, None, :qe]
        causal = jnp.arange(qs, qe)[:, None] >= jnp.arange(qe)[None, :]
        p = jax.nn.softmax(jnp.where(causal, s, -jnp.inf), axis=-1).astype(v.dtype)
        outs.append(jnp.einsum('bhqk,bkhd->bqhd', p, v[:, :qe]))
    return jnp.concatenate(outs, axis=1)


def differential_attention(q, k, v, lam):
    seq = q.shape[1]
    scale = q.shape[-1] ** -0.5
    outs = []
    for qs in range(0, seq, Q_BLOCK):
        qe = qs + Q_BLOCK
        s = jnp.einsum('bqhjd,bkhjd->bhjqk', q[:, qs:qe], k[:, :qe],
                       preferred_element_type=jnp.float32) * scale
        allowed = (jnp.arange(qs, qe)[:, None] // CHUNK) >= (jnp.arange(qe)[None, :] // CHUNK)
        p = jax.nn.softmax(jnp.where(allowed, s, -jnp.inf), axis=-1)
        p = (p[:, :, 0] - lam * p[:, :, 1]).astype(v.dtype)
        outs.append(jnp.einsum('bhqk,bkhe->bqhe', p, v[:, :qe]))
    return jnp.concatenate(outs, axis=1)


def short_conv_mixer(gate_b, gate_c, h, conv_w):
    u = gate_c * h
    y = lax.conv_general_dilated(u, conv_w[:, None, :], window_strides=(1,),
                                 padding=[(CONV_W - 1, 0)],
                                 dimension_numbers=('NWC', 'WIO', 'NWC'),
                                 feature_group_count=u.shape[-1])
    return gate_b * y


def chunk_band_attention(q, k, v, rel_table):
    bsz, seq, heads, hd = q.shape
    n_chunks = seq // CHUNK
    left = D_LEFT_CHUNKS * CHUNK
    pad = ((0, 0), (left, 0), (0, 0), (0, 0))
    kp, vp = jnp.pad(k, pad), jnp.pad(v, pad)
    rel = jnp.arange(CHUNK)[:, None] - jnp.arange(D_BAND)[None, :] + left
    rel_idx = jnp.clip(rel, -REL_CLIP, REL_CLIP) + REL_CLIP
    bias = rel_table[:, rel_idx].astype(jnp.float32)
    scale = hd ** -0.5

    def one_chunk(c):
        start = c * CHUNK
        qc = lax.dynamic_slice_in_dim(q, start, CHUNK, axis=1)
        kc = lax.dynamic_slice_in_dim(kp, start, D_BAND, axis=1)
        vc = lax.dynamic_slice_in_dim(vp, start, D_BAND, axis=1)
        s = jnp.einsum('bqhd,bkhd->bhqk', qc, kc, preferred_element_type=jnp.float32) * scale + bias
        valid = (start - left + jnp.arange(D_BAND)) >= 0
        p = jax.nn.softmax(jnp.where(valid, s, -jnp.inf), axis=-1).astype(v.dtype)
        return jnp.einsum('bhqk,bkhd->bqhd', p, vc)

    out = lax.map(one_chunk, jnp.arange(n_chunks))
    return out.transpose(1, 0, 2, 3, 4).reshape(bsz, seq, heads * hd)


def even_mixer(h, w_in, f_bias, qk_gains, lam_params, subln_gain, lambda_init, w_out, cos, sin):
    bsz, seq, _ = h.shape
    a_q, a_k, a_v, a_f, b_q, b_k, b_v = split_cols(h @ w_in, EVEN_SIZES)
    aq = rms_norm(a_q.reshape(bsz, seq, A_HEADS, HEAD_DIM), qk_gains[0])
    ak = rms_norm(a_k.reshape(bsz, seq, A_HEADS, HEAD_DIM), qk_gains[1])
    av = a_v.reshape(bsz, seq, A_HEADS, HEAD_DIM)
    logf = jax.nn.log_sigmoid((a_f + f_bias).astype(jnp.float32))
    cum_logf = jnp.cumsum(logf, axis=1).transpose(0, 2, 1)
    a_out = forgetting_attention(aq, ak, av, cum_logf).reshape(bsz, seq, A_WIDTH)
    bq = rms_norm(b_q.reshape(bsz, seq, B_HEADS, 2, HEAD_DIM), qk_gains[2])
    bk = rms_norm(b_k.reshape(bsz, seq, B_HEADS, 2, HEAD_DIM), qk_gains[3])
    bq, bk = apply_partial_rope(bq, cos, sin), apply_partial_rope(bk, cos, sin)
    bv = b_v.reshape(bsz, seq, B_HEADS, B_VDIM)
    lp = lam_params.astype(jnp.float32)
    lam = jnp.exp(jnp.sum(lp[0] * lp[1])) - jnp.exp(jnp.sum(lp[2] * lp[3])) + lambda_init
    b_out = differential_attention(bq, bk, bv, lam)
    b_out = (rms_norm(b_out, subln_gain) * (1.0 - lambda_init)).reshape(bsz, seq, B_WIDTH)
    return jnp.concatenate([a_out, b_out], axis=-1) @ w_out


def odd_mixer(h, w_in, conv_w, qk_gains, rel_table, w_out):
    bsz, seq, _ = h.shape
    c_b, c_c, c_h, d_q, d_k, d_v = split_cols(h @ w_in, ODD_SIZES)
    c_out = short_conv_mixer(c_b, c_c, c_h, conv_w)
    dq = rms_norm(d_q.reshape(bsz, seq, D_HEADS, HEAD_DIM), qk_gains[0])
    dk = rms_norm(d_k.reshape(bsz, seq, D_HEADS, HEAD_DIM), qk_gains[1])
    dv = d_v.reshape(bsz, seq, D_HEADS, HEAD_DIM)
    d_out = chunk_band_attention(dq, dk, dv, rel_table)
    return jnp.concatenate([c_out, d_out], axis=-1) @ w_out


def memory_cross_attention(h, mem_n, w_q, w_kv, qk_gains, w_o):
    bsz, seq, _ = h.shape
    q = rms_norm((h @ w_q).reshape(bsz, seq, X_HEADS, X_HEAD_DIM), qk_gains[0])
    k, v = split_cols(mem_n @ w_kv, (D_MODEL, D_MODEL))
    k = rms_norm(k.reshape(bsz, -1, X_HEADS, X_HEAD_DIM), qk_gains[1])
    v = v.reshape(bsz, -1, X_HEADS, X_HEAD_DIM)
    s = jnp.einsum('bqhd,bkhd->bhqk', q, k, preferred_element_type=jnp.float32) * X_HEAD_DIM ** -0.5
    p = jax.nn.softmax(s, axis=-1).astype(v.dtype)
    o = jnp.einsum('bhqk,bkhd->bqhd', p, v).reshape(bsz, seq, D_MODEL)
    return o @ w_o


def setup_inputs(seed: int = 0) -> dict:
    key = jax.random.key(seed)
    ks = jax.random.split(key, 24)
    f32 = jnp.float32

    def dense(k, shape):
        return jax.random.normal(k, shape, f32) * shape[-2] ** -0.5

    def gain(k, shape):
        return 1.0 + 0.05 * jax.random.normal(k, shape, f32)

    x = jax.random.normal(ks[0], (BATCH, SEQ, D_MODEL), f32)
    mem = jax.random.normal(ks[1], (BATCH, N_MEM, D_MODEL), f32)
    offset = jax.random.randint(ks[2], (BATCH, 1), 0, MAX_POS_OFFSET, dtype=jnp.int32)
    positions = offset + jnp.arange(SEQ, dtype=jnp.int32)[None, :]
    return {
        'x': x,
        'mem': mem,
        'positions': positions,
        'ln_gains': gain(ks[3], (DEPTH, 5, D_MODEL)),
        'ffn1_w_in': dense(ks[4], (DEPTH, D_MODEL, 2 * D_FF)),
        'ffn1_w_out': dense(ks[5], (DEPTH, D_FF, D_MODEL)),
        'ffn2_w_in': dense(ks[6], (DEPTH, D_MODEL, 2 * D_FF)),
        'ffn2_w_out': dense(ks[7], (DEPTH, D_FF, D_MODEL)),
        'even_w_in': dense(ks[8], (N_EVEN, D_MODEL, EVEN_IN)),
        'even_f_bias': jax.random.uniform(ks[9], (N_EVEN, A_HEADS), f32, 1.0, 5.0),
        'even_qk_gains': gain(ks[10], (N_EVEN, 4, HEAD_DIM)),
        'even_lambda': 0.1 * jax.random.normal(ks[11], (N_EVEN, 4, HEAD_DIM), f32),
        'even_subln_gain': gain(ks[12], (N_EVEN, B_VDIM)),
        'even_w_out': dense(ks[13], (N_EVEN, EVEN_MIX, D_MODEL)),
        'odd_w_in': dense(ks[14], (N_ODD, D_MODEL, ODD_IN)),
        'odd_conv_w': jax.random.normal(ks[15], (N_ODD, CONV_W, C_WIDTH), f32) * CONV_W ** -0.5,
        'odd_qk_gains': gain(ks[16], (N_ODD, 2, HEAD_DIM)),
        'odd_rel_bias': 0.5 * jax.random.normal(ks[17], (N_ODD, D_HEADS, 2 * REL_CLIP + 1), f32),
        'odd_w_out': dense(ks[18], (N_ODD, ODD_MIX, D_MODEL)),
        'x_w_q': dense(ks[19], (DEPTH, D_MODEL, D_MODEL)),
        'x_w_kv': dense(ks[20], (DEPTH, D_MODEL, 2 * D_MODEL)),
        'x_qk_gains': gain(ks[21], (DEPTH, 2, X_HEAD_DIM)),
        'x_w_o': dense(ks[22], (DEPTH, D_MODEL, D_MODEL)),
    }


def reference(x, mem, positions, ln_gains, ffn1_w_in, ffn1_w_out, ffn2_w_in, ffn2_w_out,
              even_w_in, even_f_bias, even_qk_gains, even_lambda, even_subln_gain, even_w_out,
              odd_w_in, odd_conv_w, odd_qk_gains, odd_rel_bias, odd_w_out,
              x_w_q, x_w_kv, x_qk_gains, x_w_o):
    cos, sin = rope_tables(positions)
    for layer in range(DEPTH):
        g = ln_gains[layer]
        x = x + 0.5 * swiglu(rms_norm(x, g[0]), ffn1_w_in[layer], ffn1_w_out[layer])
        h = rms_norm(x, g[1])
        if layer % 2 == 0:
            e = layer // 2
            lambda_init = 0.8 - 0.6 * math.exp(-0.3 * layer)
            mixed = even_mixer(h, even_w_in[e], even_f_bias[e], even_qk_gains[e], even_lambda[e],
                               even_subln_gain[e], lambda_init, even_w_out[e], cos, sin)
        else:
            o = layer // 2
            mixed = odd_mixer(h, odd_w_in[o], odd_conv_w[o], odd_qk_gains[o], odd_rel_bias[o],
                              odd_w_out[o])
        x = x + mixed
        x = x + memory_cross_attention(rms_norm(x, g[2]), rms_norm(mem, g[3]), x_w_q[layer],
                                       x_w_kv[layer], x_qk_gains[layer], x_w_o[layer])
        x = x + 0.5 * swiglu(rms_norm(x, g[4]), ffn2_w_in[layer], ffn2_w_out[layer])
    return x
```

```python
import math
import numpy as np
import concourse.bass as bass
import concourse.mybir as mybir
from concourse.bass_utils import run_bass_kernel_spmd

F32 = mybir.dt.float32
BF16 = mybir.dt.bfloat16
I32 = mybir.dt.int32
AF = mybir.ActivationFunctionType
ALU = mybir.AluOpType

D_MODEL = 1024
SEQ = 2048
DEPTH = 4
D_FF = 2816
N_MEM = 256
NCORES = 8
RMS_EPS = 1e-6
KD = D_MODEL // 128
NTT = SEQ // 512
NTC = SEQ // 128
NFC = D_FF // 128
EVEN_IN = 3080
ODD_IN = 3072

ENGS = ("pe", "act", "dve", "pool", "sp")
TRUST_INORDER = {"pe": True, "act": False, "dve": False, "pool": False, "sp": True}


class Atom:
    __slots__ = ("lw", "rd", "name")

    def __init__(self, name=""):
        self.lw = None
        self.rd = []
        self.name = name


class Op:
    __slots__ = ("eng", "fn", "waits_e", "waits_d", "signal", "phase", "dma", "snap", "ms")

    def __init__(self, eng, fn, phase):
        self.eng = eng
        self.fn = fn
        self.waits_e = {}
        self.waits_d = {}
        self.signal = False
        self.phase = phase
        self.dma = None
        self.snap = None
        self.ms = None


class K:
    DEBUG_LOG = False
    LAST = None

    def __init__(self):
        self.ops = {e: [] for e in ENGS}
        self.seen = {e: {f: -1 for f in ENGS} for e in ENGS}
        self.seen_d = {e: {} for e in ENGS}
        self.dma_cnt = {}
        self.phase = 0
        self.log = []

    def new_phase(self):
        self.phase += 1

    def _need(self, eng, tok, de, dd):
        if tok is None:
            return
        if tok[0] == "e":
            f, idx = tok[1], tok[2]
            if f == eng and TRUST_INORDER[eng]:
                return
            if idx <= self.seen[eng][f]:
                return
            if idx > de.get(f, -1):
                de[f] = idx
        else:
            key, cnt = tok[1], tok[2]
            if cnt <= self.seen_d[eng].get(key, 0):
                return
            if cnt > dd.get(key, 0):
                dd[key] = cnt

    def op(self, eng, fn, reads=(), writes=(), dma_key=None):
        de, dd = {}, {}
        for a in reads:
            self._need(eng, a.lw, de, dd)
        for a in writes:
            self._need(eng, a.lw, de, dd)
            for t in a.rd:
                self._need(eng, t, de, dd)
        o = Op(eng, fn, self.phase)
        o.waits_e = de
        o.waits_d = dd
        seen = self.seen[eng]
        for f, idx in de.items():
            src = self.ops[f][idx]
            src.signal = True
            if idx > seen[f]:
                seen[f] = idx
            if src.snap is not None:
                for g, v in src.snap.items():
                    if v > seen[g]:
                        seen[g] = v
        sd = self.seen_d[eng]
        for key, cnt in dd.items():
            if cnt > sd.get(key, 0):
                sd[key] = cnt
        idx = len(self.ops[eng])
        self.ops[eng].append(o)
        if K.DEBUG_LOG:
            self.log.append((eng, idx, o, tuple(reads), tuple(writes)))
        seen[eng] = idx if TRUST_INORDER[eng] else seen[eng]
        o.snap = dict(seen)
        if dma_key is not None:
            c = self.dma_cnt.get(dma_key, 0) + 16
            self.dma_cnt[dma_key] = c
            o.dma = dma_key
            tok = ("d", dma_key, c)
        else:
            tok = ("e", eng, idx)
        for a in reads:
            rd = a.rd
            if tok[0] == "e":
                for i, t in enumerate(rd):
                    if t[0] == "e" and t[1] == eng:
                        rd[i] = tok
                        break
                else:
                    rd.append(tok)
            else:
                for i, t in enumerate(rd):
                    if t[0] == "d" and t[1] == tok[1]:
                        rd[i] = tok
                        break
                else:
                    rd.append(tok)
        for a in writes:
            a.lw = tok
            a.rd = []
        return tok

    def emit(self, nc):
        K.LAST = self
        nph = self.phase + 1
        from contextlib import ExitStack
        with ExitStack() as es:
            esems = {}
            for e in ENGS:
                phases = sorted({o.phase for o in self.ops[e] if o.signal})
                for ph in phases:
                    esems[(e, ph)] = es.enter_context(nc.semaphore("s_%s_%d" % (e, ph)))
            dsems = {}
            for key in self.dma_cnt:
                dsems[key] = es.enter_context(nc.semaphore("d_%s" % (key,)))
            for e in ENGS:
                cnt = {}
                for o in self.ops[e]:
                    if o.signal:
                        c = cnt.get(o.phase, 0) + 1
                        cnt[o.phase] = c
                        o.ms = (esems[(e, o.phase)], c)
            block = es.enter_context(nc.Block())
            ops = self.ops

            def run(e, name):
                for o in ops[name]:
                    for f, idx in o.waits_e.items():
                        sem, val = ops[f][idx].ms
                        e.wait_ge(sem, val)
                    for key, cnt_ in o.waits_d.items():
                        e.wait_ge(dsems[key], cnt_)
                    if o.fn is None:
                        continue
                    ins = o.fn(e)
                    if o.dma is not None:
                        ins.then_inc(dsems[o.dma], 16)
                    elif o.signal:
                        ins.then_inc(o.ms[0], 1)

            @block.tensor
            def _(e):
                run(e, "pe")

            @block.scalar
            def _(e):
                run(e, "act")

            @block.vector
            def _(e):
                run(e, "dve")

            @block.gpsimd
            def _(e):
                run(e, "pool")

            @block.sync
            def _(e):
                run(e, "sp")
        print("ops:", {e: len(self.ops[e]) for e in ENGS}, "sems:", len(esems) + len(dsems))


class Cfg:
    def __init__(self, nseq=4, layers=(0, 1, 2, 3), do_ffn1=True, do_mixer=True, do_xattn=True, do_ffn2=True,
                 units=None):
        self.nseq = nseq
        self.layers = tuple(layers)
        self.do_ffn1 = do_ffn1
        self.do_mixer = do_mixer
        self.do_xattn = do_xattn
        self.do_ffn2 = do_ffn2
        self.units = units


C_GAINS = 0
C_EG = C_GAINS + DEPTH * 5 * KD
C_OG = C_EG + 8
C_XG = C_OG + 4
C_FB = C_XG + 16
C_CW = C_FB + 2
C_INV = C_CW + 24
C_LAM = C_INV + 1
C_SUBG = C_LAM + 512
C_ID = C_SUBG + 256
C_TRI = C_ID + 128
C_BD = C_TRI + 128
C_ROT = C_BD + 128
NCF = C_ROT + 128
EXT_LEN = 769
TWO_PI = 2.0 * math.pi
CW1 = 6.28125
CW2 = TWO_PI - CW1
MAGIC = 12582912.0


class Region:
    def __init__(self, t, nbytes):
        self.t = t
        self.atoms = [Atom() for _ in range(nbytes // 512)]

    def at(self, off, nbytes):
        return self.atoms[off // 512:(off + nbytes + 511) // 512]

    def view(self, off, nbytes, dt=None):
        ap = self.t[:, off // 2:(off + nbytes) // 2]
        if dt is not None:
            ap = ap.bitcast(dt)
        return ap


def build_program(cfg):
    nc = bass.Bass("TRN2", target_bir_lowering=False)
    k = K()
    from contextlib import ExitStack
    es = ExitStack()
    nseq = cfg.nseq

    def dram_in(name, shape, dt=F32):
        return nc.dram_tensor(name, list(shape), dt, kind="ExternalInput").ap()

    xT_d = dram_in("xT", [nseq, D_MODEL, SEQ])
    memT_d = dram_in("memT", [nseq, D_MODEL, N_MEM])
    pos_d = dram_in("pos", [nseq, SEQ], I32)
    outT_d = nc.dram_tensor("outT", [nseq, D_MODEL, SEQ], F32, kind="ExternalOutput").ap()
    cf_d = dram_in("cf32", [128, NCF])
    ext_d = dram_in("relext", [2, 8, 128, 640])
    f1in_d = dram_in("ffn1_w_in", [DEPTH, D_MODEL, 2 * D_FF])
    f1out_d = dram_in("ffn1_w_out", [DEPTH, D_FF, D_MODEL])
    f2in_d = dram_in("ffn2_w_in", [DEPTH, D_MODEL, 2 * D_FF])
    f2out_d = dram_in("ffn2_w_out", [DEPTH, D_FF, D_MODEL])
    ewin_d = dram_in("even_w_in", [2, D_MODEL, EVEN_IN])
    ewout_d = dram_in("even_w_out", [2, D_MODEL, D_MODEL])
    owin_d = dram_in("odd_w_in", [2, D_MODEL, ODD_IN])
    owout_d = dram_in("odd_w_out", [2, D_MODEL, D_MODEL])
    xwq_d = dram_in("x_w_q", [DEPTH, D_MODEL, D_MODEL])
    xwkv_d = dram_in("x_w_kv", [DEPTH, D_MODEL, 2 * D_MODEL])
    xwo_d = dram_in("x_w_o", [DEPTH, D_MODEL, D_MODEL])
    fparts_d = nc.dram_tensor("fparts", [8, 3, SEQ], BF16, kind="Internal").ap()
    a_fparts = Atom("fparts")

    def sb(name, shape, dt):
        return es.enter_context(nc.sbuf_tensor(name, list(shape), dt))

    xT = sb("xT_sb", [128, KD, SEQ], F32)
    hT = sb("hT_sb", [128, KD, SEQ], BF16)
    big_t = sb("big_sb", [128, 16384], BF16)
    r2_t = sb("r2_sb", [128, 8192], BF16)
    big = Region(big_t, 32768)
    r2 = Region(r2_t, 16384)
    NWI, NWO = 2, 10
    WI = [sb("wi%d" % i, [128, KD, 392], BF16) for i in range(NWI)]
    WO = [sb("wo%d" % i, [128, 1024], BF16) for i in range(NWO)]
    wf = sb("wf_sb", [128, KD, 128], BF16)
    cf = sb("cf_sb", [128, NCF], F32)
    ones_bf = sb("ones_bf", [128, 128], BF16)
    id_bf = sb("id_bf", [128, 128], BF16)
    tri_bf = sb("tri_bf", [128, 128], BF16)
    bd_bf = sb("bd_bf", [128, 128], BF16)
    rot_bf = sb("rot_bf", [128, 128], BF16)
    onesf = sb("onesf", [128, 512], F32)
    negfb = sb("negfb", [128, 2], F32)
    lamt = sb("lamt", [128, 8], F32)
    subgs = sb("subgs", [128, 256], F32)
    negF = sb("negF", [128, NTC, 8], F32)
    sqt = [sb("sqt%d" % i, [128, 512], BF16) for i in range(2)]
    rt = [sb("rt%d" % i, [128, 512], F32) for i in range(2)]
    sg = [sb("sg%d" % i, [128, 512], F32) for i in range(2)]
    NPT = 6
    PT = [sb("pt%d" % i, [128, 512], BF16) for i in range(NPT)]
    sm = sb("sm_sb", [128, 128], F32)
    PS = [es.enter_context(nc.psum_tensor("ps%d" % i, [128, 512], F32)) for i in range(8)]

    a_x = [[Atom() for tt in range(NTT)] for kk in range(KD)]
    a_h = [[Atom() for tt in range(NTT)] for kk in range(KD)]
    a_WI = [Atom() for _ in range(NWI)]
    a_WO = [Atom() for _ in range(NWO)]
    a_wf = Atom()
    a_ps = [Atom() for i in range(8)]
    K.PS_ATOMS = {a: 'ps%d' % i for i, a in enumerate(a_ps)}
    a_cf = Atom()
    a_const = Atom()
    a_par = Atom()
    a_negF = Atom()
    a_sqt = [Atom() for _ in range(2)]
    a_rt = [Atom() for _ in range(2)]
    a_sg = [Atom() for _ in range(2)]
    a_PT = [Atom() for _ in range(NPT)]
    a_sm = Atom()
    a_smg = [Atom() for _ in range(4)]
    SB_RING = (2, 3, 6, 7)
    PB_RING = (0, 1, 4, 5)
    st = {"wi": 0, "wo": 0, "sq": 0, "r": 0, "sg": 0, "pt": 0, "sb": 0, "acc": 0, "py": 0, "nb": 0, "sbr": 0, "pb": 0, "smg": 0}

    def nxt(key, n):
        v = st[key] % n
        st[key] += 1
        return v

    def mm(pi, out, lhsT, rhs, s, p, reads, skip=False):
        if skip:
            k.op("pe", lambda e: e.matmul(out, lhsT, rhs, start=s, stop=p, skip_group_check=True), reads=reads,
                 writes=[a_ps[pi]])
        else:
            k.op("pe", lambda e: e.matmul(out, lhsT, rhs, start=s, stop=p), reads=reads, writes=[a_ps[pi]])

    def act(out, in_, func, reads, writes, **kw):
        k.op("act", lambda e: e.activation(out, in_, func, **kw), reads=reads, writes=writes)

    def tt_(out, in0, in1, op, reads, writes, eng="dve"):
        k.op(eng, lambda e: e.tensor_tensor(out=out, in0=in0, in1=in1, op=op), reads=reads, writes=writes)

    def ts_(out, in0, s1, s2, op0, op1, reads, writes, eng="dve"):
        if s2 is None:
            k.op(eng, lambda e: e.tensor_scalar(out, in0, s1, None, op0), reads=reads, writes=writes)
        else:
            k.op(eng, lambda e: e.tensor_scalar(out, in0, s1, s2, op0, op1), reads=reads, writes=writes)

    def stt(out, in0, scalar, in1, op0, op1, reads, writes):
        k.op("dve", lambda e: e.scalar_tensor_tensor(out=out, in0=in0, scalar=scalar, in1=in1, op0=op0, op1=op1),
             reads=reads, writes=writes)

    def cpy(out, in_, reads, writes, eng="dve"):
        k.op(eng, lambda e: e.tensor_copy(out, in_), reads=reads, writes=writes)

    def mset(ap, val, writes, eng="dve"):
        k.op(eng, lambda e: e.memset(ap, val), writes=writes)

    def recip(out, in_, reads, writes):
        k.op("dve", lambda e: e.reciprocal(out, in_), reads=reads, writes=writes)

    def dma(eng, out, in_, reads, writes, key):
        return k.op(eng, lambda e: e.dma_start(out=out, in_=in_), reads=reads, writes=writes, dma_key=key)

    def tsl(tt):
        return slice(tt * 512, (tt + 1) * 512)

    def csl(c):
        return slice(c * 128, (c + 1) * 128)

    def cfc(c, n=1):
        return cf[:, c:c + n]

    dma("sp", cf[:], cf_d[:, :], [], [a_cf], "setup")
    mset(ones_bf[:], 1.0, [a_const], eng="pool")
    mset(onesf[:], 1.0, [a_const], eng="pool")
    for (dst, c0) in ((id_bf, C_ID), (tri_bf, C_TRI), (bd_bf, C_BD), (rot_bf, C_ROT)):
        cpy(dst[:], cf[:, c0:c0 + 128], [a_cf], [a_const])
    ts_(negfb[:], cf[:, C_FB:C_FB + 2], -1.0, None, ALU.mult, None, [a_cf], [a_par])
    for e_ in range(2):
        lay = 2 * e_
        lam_init = 0.8 - 0.6 * math.exp(-0.3 * lay)
        base = C_LAM + e_ * 256
        for j in range(2):
            tt_(sg[0][:, 0:64], cf[:, base + (2 * j) * 64:base + (2 * j + 1) * 64],
                cf[:, base + (2 * j + 1) * 64:base + (2 * j + 2) * 64], ALU.mult, [a_cf], [a_sg[0]])
            k.op("dve", lambda e, j=j: e.reduce_sum(sm[:, 104 + j:105 + j], sg[0][:, 0:64], mybir.AxisListType.X),
                 reads=[a_sg[0]], writes=[a_sm])
        act(sm[:, 106:108], sm[:, 104:106], AF.Exp, [a_sm], [a_sm])
        tt_(sm[:, 108:109], sm[:, 106:107], sm[:, 107:108], ALU.subtract, [a_sm], [a_sm])
        ts_(lamt[:, e_:e_ + 1], sm[:, 108:109], lam_init, -1.0, ALU.add, ALU.mult, [a_sm], [a_par])
        ts_(subgs[:, e_ * 128:(e_ + 1) * 128], cf[:, C_SUBG + e_ * 128:C_SUBG + (e_ + 1) * 128], 1.0 - lam_init, None,
            ALU.mult, None, [a_cf], [a_par])
    mset(wf[:], 0.0, [a_wf], eng="pool")

    def gcol(l, i, kk):
        return cfc(C_GAINS + (l * 5 + i) * KD + kk)

    def ss1(srcs, lhs_ones, N):
        pb = 6 + nxt("nb", 2)
        n = len(srcs)
        for i, (ap, ra) in enumerate(srcs):
            s = nxt("sq", 2)
            act(sqt[s][:, 0:N], ap, AF.Square, ra, [a_sqt[s]])
            mm(pb, PS[pb][:, 0:N], lhs_ones, sqt[s][:, 0:N], i == 0, i == n - 1, [a_sqt[s], a_const])
        return pb

    def ss2(pb, N, inv_n):
        r = nxt("r", 2)
        act(rt[r][:, 0:N], PS[pb][:, 0:N], AF.Ln, [a_ps[pb]], [a_rt[r]], bias=RMS_EPS, scale=inv_n)
        act(rt[r][:, 0:N], rt[r][:, 0:N], AF.Exp, [a_rt[r]], [a_rt[r]], scale=-0.5)
        return r

    def sumsq_rstd(srcs, lhs_ones, N, inv_n):
        return ss2(ss1(srcs, lhs_ones, N), N, inv_n)

    hook = {"after": None, "next": None, "prenorm": None}

    def rmsnorm_tt(l, i, tt):
        r = sumsq_rstd([(xT[:, kk, tsl(tt)], [a_x[kk][tt]]) for kk in range(KD)], ones_bf[:], 512, 1.0 / D_MODEL)
        for kk in range(KD):
            stt(hT[:, kk, tsl(tt)], xT[:, kk, tsl(tt)], gcol(l, i, kk), rt[r][:], ALU.mult, ALU.mult,
                [a_x[kk][tt], a_rt[r], a_cf], [a_h[kk][tt]])

    def rmsnorm_x(l, i):
        if hook["prenorm"] == (l, i):
            hook["prenorm"] = None
            return
        for tt in range(NTT):
            rmsnorm_tt(l, i, tt)

    def arm_after():
        key = hook["next"]
        if key is None:
            hook["after"] = None
            return

        def after(tt):
            rmsnorm_tt(key[0], key[1], tt)
            if tt == NTT - 1:
                hook["prenorm"] = key
        hook["after"] = after

    def load_wi(src_l, colgroups):
        wi = nxt("wi", NWI)
        off = 0
        for (c0, n) in colgroups:
            dma("pool", WI[wi][:, :, off:off + n], src_l[:, :, c0:c0 + n], [], [a_WI[wi]], "wi%d" % wi)
            off += n
        return wi

    def load_wo(src_rows):
        wo = nxt("wo", NWO)
        dma("pool", WO[wo][:], src_rows, [], [a_WO[wo]], "wo%d" % wo)
        return wo

    def x_update(o, tt, py, half):
        if half:
            stt(xT[:, o, tsl(tt)], PS[py][:], 0.5, xT[:, o, tsl(tt)], ALU.mult, ALU.add,
                [a_ps[py], a_x[o][tt]], [a_x[o][tt]])
        else:
            tt_(xT[:, o, tsl(tt)], PS[py][:], xT[:, o, tsl(tt)], ALU.add, [a_ps[py], a_x[o][tt]], [a_x[o][tt]])

    def big_chunk(ci, tt):
        off = ci * 4096 + tt * 1024
        return big.view(off, 1024), big.at(off, 1024)

    def out_proj(chunks, half):
        n = len(chunks)
        after = hook["after"]
        hook["after"] = None
        for tt in range(NTT):
            for o in range(KD):
                py = 4 + nxt("py", 2)
                for ci, (wo, apf, atf) in enumerate(chunks):
                    mm(py, PS[py][:], WO[wo][:, csl(o)], apf(tt), ci == 0, ci == n - 1, [a_WO[wo]] + atf(tt))
                x_update(o, tt, py, half)
            if after is not None and tt >= 1:
                after(tt - 1)
        if after is not None:
            after(NTT - 1)

    def ffn(l, which, win_d, wout_d):
        rmsnorm_x(l, 0 if which == 1 else 4)
        groups = [list(range(0, 8)), list(range(8, 15)), list(range(15, 22))]
        win_l = win_d[l].rearrange("(k p) f -> p k f", p=128)
        for grp in groups:
            chunks = []
            for ci, c in enumerate(grp):
                wi = load_wi(win_l, [(c * 128, 128), (D_FF + c * 128, 128)])
                wo = load_wo(wout_d[l, c * 128:(c + 1) * 128, :])
                chunks.append((wo, (lambda tt, ci=ci: big_chunk(ci, tt)[0]), (lambda tt, ci=ci: big_chunk(ci, tt)[1])))
                for tt in range(NTT):
                    pg = (tt % 2) * 2
                    pu = pg + 1
                    for kk in range(KD):
                        mm(pg, PS[pg][:], WI[wi][:, kk, 0:128], hT[:, kk, tsl(tt)], kk == 0, kk == KD - 1,
                           [a_WI[wi], a_h[kk][tt]])
                    for kk in range(KD):
                        mm(pu, PS[pu][:], WI[wi][:, kk, 128:256], hT[:, kk, tsl(tt)], kk == 0, kk == KD - 1,
                           [a_WI[wi], a_h[kk][tt]])
                    s = nxt("sg", 2)
                    act(sg[s][:], PS[pg][:], AF.Silu, [a_ps[pg]], [a_sg[s]])
                    bap, bat = big_chunk(ci, tt)
                    tt_(bap, PS[pu][:], sg[s][:], ALU.mult, [a_ps[pu], a_sg[s]], bat)
            if grp is groups[-1]:
                arm_after()
            out_proj(chunks, True)

    def xattn(b, l):
        memT = r2.view(0, 8192, F32).rearrange("p (k m) -> p k m", m=N_MEM)
        a_mem = r2.at(0, 8192)
        KT = r2.view(0, 4096).rearrange("p (h c m) -> p h c m", h=4, c=2)
        a_KT = r2.at(0, 4096)
        Vt = r2.view(4096, 4096).rearrange("p (mc v) -> p mc v", mc=2)
        a_Vt = r2.at(4096, 4096)
        mnT = r2.view(8192, 4096).rearrange("p (k m) -> p k m", m=N_MEM)
        a_mn = r2.at(8192, 4096)
        qTs = [r2.view(12288 + i * 2048, 2048).rearrange("p (c t) -> p c t", c=2) for i in range(2)]
        a_qT = [r2.at(12288 + i * 2048, 2048) for i in range(2)]
        dma("sp", memT, memT_d[b].rearrange("(k p) m -> p k m", p=128), [], a_mem, "mem")
        r = sumsq_rstd([(memT[:, kk, :], a_mem) for kk in range(KD)], ones_bf[:], N_MEM, 1.0 / D_MODEL)
        for kk in range(KD):
            stt(mnT[:, kk, :], memT[:, kk, :], gcol(l, 3, kk), rt[r][:, 0:N_MEM], ALU.mult, ALU.mult,
                a_mem + [a_rt[r], a_cf], a_mn)
        wkv_l = xwkv_d[l].rearrange("(k p) f -> p k f", p=128)
        for h in range(4):
            wi = load_wi(wkv_l, [(h * 256, 256)])
            pk = nxt("sb", 2)
            for c in range(2):
                for kk in range(KD):
                    mm(pk, PS[pk][:, c * 256:(c + 1) * 256], WI[wi][:, kk, csl(c)], mnT[:, kk, :], kk == 0, kk == KD - 1,
                       [a_WI[wi]] + a_mn)
            r = sumsq_rstd([(PS[pk][:, c * 256:(c + 1) * 256], [a_ps[pk]]) for c in range(2)], ones_bf[:], 256, 1.0 / 256)
            for c in range(2):
                stt(KT[:, h, c, :], PS[pk][:, c * 256:(c + 1) * 256], cfc(C_XG + (l * 2 + 1) * 2 + c), rt[r][:, 0:256],
                    ALU.mult, ALU.mult, [a_ps[pk], a_rt[r], a_cf], a_KT)
        for h in range(4):
            wi = load_wi(wkv_l, [(D_MODEL + h * 256, 256)])
            pk = nxt("sb", 2)
            for mc in range(2):
                for kk in range(KD):
                    mm(pk, PS[pk][:, mc * 256:(mc + 1) * 256], mnT[:, kk, csl(mc)], WI[wi][:, kk, 0:256], kk == 0,
                       kk == KD - 1, [a_WI[wi]] + a_mn)
            for mc in range(2):
                cpy(Vt[:, mc, h * 256:(h + 1) * 256], PS[pk][:, mc * 256:(mc + 1) * 256], [a_ps[pk]], a_Vt)
        rmsnorm_x(l, 2)
        wq_l = xwq_d[l].rearrange("(k p) f -> p k f", p=128)
        scale = 1.0 / 16.0
        for h in range(4):
            wi = load_wi(wq_l, [(h * 256, 256)])
            for tt in range(NTT):
                qi = nxt("acc", 2)
                qT = qTs[qi]
                for c in range(2):
                    for kk in range(KD):
                        mm(c, PS[c][:], WI[wi][:, kk, csl(c)], hT[:, kk, tsl(tt)], kk == 0, kk == KD - 1,
                           [a_WI[wi], a_h[kk][tt]])
                r = sumsq_rstd([(PS[c][:], [a_ps[c]]) for c in range(2)], ones_bf[:], 512, 1.0 / 256)
                for c in range(2):
                    stt(qT[:, c, :], PS[c][:], cfc(C_XG + (l * 2 + 0) * 2 + c), rt[r][:], ALU.mult, ALU.mult,
                        [a_ps[c], a_rt[r], a_cf], a_qT[qi])
                pts = []
                for mc in range(2):
                    sbk = 2 + mc
                    for c in range(2):
                        mm(sbk, PS[sbk][:], KT[:, h, c, csl(mc)], qT[:, c, :], c == 0, c == 1, a_KT + a_qT[qi])
                    pt = nxt("pt", NPT)
                    act(PT[pt][:], PS[sbk][:], AF.Exp, [a_ps[sbk]], [a_PT[pt]], scale=scale)
                    pts.append(pt)
                pb = 6 + nxt("nb", 2)
                for mc in range(2):
                    mm(pb, PS[pb][:], ones_bf[:], PT[pts[mc]][:], mc == 0, mc == 1, [a_const, a_PT[pts[mc]]])
                r = nxt("r", 2)
                recip(rt[r][:], PS[pb][:], [a_ps[pb]], [a_rt[r]])
                for dvc in range(2):
                    po = 4 + dvc
                    for mc in range(2):
                        mm(po, PS[po][:], Vt[:, mc, h * 256 + dvc * 128:h * 256 + (dvc + 1) * 128], PT[pts[mc]][:],
                           mc == 0, mc == 1, a_Vt + [a_PT[pts[mc]]])
                    bap, bat = big_chunk(h * 2 + dvc, tt)
                    tt_(bap, PS[po][:], rt[r][:], ALU.mult, [a_ps[po], a_rt[r]], bat)
        chunks = []
        for c in range(8):
            wo = load_wo(xwo_d[l, c * 128:(c + 1) * 128, :])
            chunks.append((wo, (lambda tt, c=c: big_chunk(c, tt)[0]), (lambda tt, c=c: big_chunk(c, tt)[1])))
        arm_after()
        out_proj(chunks, False)

    Q0 = big.view(0, 4096)
    Q1 = big.view(4096, 4096)
    KP0 = big.view(8192, 4096)
    KP1 = big.view(12288, 4096)
    a_Q0 = [big.at(0 + tt * 1024, 1024) for tt in range(NTT)]
    a_Q1 = [big.at(4096 + tt * 1024, 1024) for tt in range(NTT)]
    a_KP0 = big.at(8192, 4096)
    a_KP1 = big.at(12288, 4096)
    VOFF = 16384
    vtok = big.view(VOFF, 16 * 130 * 2)
    vtok4 = vtok.rearrange("p (c h n) -> p c h n", h=2, n=65)
    vtok3 = vtok.rearrange("p (c n) -> p c n", n=130)
    a_vtok = big.at(VOFF, 16 * 130 * 2)
    MOFF = 20992
    mtok = big.view(MOFF, 4096).rearrange("p (c n) -> p c n", n=128)
    a_mtok = big.at(MOFF, 4096)
    XOFF = 25088
    mixT = big.view(XOFF, 4096)
    a_mixT = [big.at(XOFF + tt * 1024, 1024) for tt in range(NTT)]
    YOFF = 29184
    y32 = big.view(YOFF, 2048, F32)
    a_y32 = big.at(YOFF, 2048)
    OOFF = 31232
    o32 = [big.view(OOFF + i * 512, 512, F32) for i in range(2)]
    a_o32 = [big.at(OOFF + i * 512, 512) for i in range(2)]

    def proj_fm(wi, col0, tt, pi):
        for kk in range(KD):
            mm(pi, PS[pi][:], WI[wi][:, kk, col0:col0 + 128], hT[:, kk, tsl(tt)], kk == 0, kk == KD - 1,
               [a_WI[wi], a_h[kk][tt]])

    def proj_v_tok(wi, col0, kind):
        if kind == "B":
            mset(vtok3[:, :, 128:129], 1.0, a_vtok)
        else:
            mset(vtok4[:, :, :, 64:65], 1.0, a_vtok)
        for g in range(4):
            pv = nxt("sb", 2)
            for c4 in range(4):
                tc = g * 4 + c4
                for kk in range(KD):
                    mm(pv, PS[pv][:, csl(c4)], hT[:, kk, csl(tc)], WI[wi][:, kk, col0:col0 + 128], kk == 0, kk == KD - 1,
                       [a_WI[wi], a_h[kk][tc // 4]])
            if kind == "B":
                cpy(vtok3[:, g * 4:(g + 1) * 4, 0:128], PS[pv][:].rearrange("p (c n) -> p c n", n=128), [a_ps[pv]], a_vtok)
            else:
                cpy(vtok4[:, g * 4:(g + 1) * 4, :, 0:64], PS[pv][:].rearrange("p (c h n) -> p c h n", h=2, n=64),
                    [a_ps[pv]], a_vtok)

    def unit_out(wo):
        for tt in range(NTT):
            pb = 6 + nxt("nb", 2)
            psb = PS[pb][:].bitcast(BF16)
            for c4 in range(4):
                k.op("pe", lambda e, c4=c4, tt=tt, psb=psb: e.transpose(psb[:, csl(c4)], mtok[:, tt * 4 + c4, :], id_bf[:]),
                     reads=a_mtok + [a_const], writes=[a_ps[pb]])
            cpy(mixT[:, tsl(tt)], psb[:, 0:512], [a_ps[pb]], a_mixT[tt])
        out_proj([(wo, (lambda tt: mixT[:, tsl(tt)]), (lambda tt: a_mixT[tt]))], False)

    def headnorm_pair(pi, tt, gcol_ap, out0, out0_at, out1, out1_at, f32out=None, r=None):
        if r is None:
            r = sumsq_rstd([(PS[pi][:], [a_ps[pi]])], bd_bf[:], 512, 1.0 / 64)
        if f32out is not None:
            stt(f32out[0], PS[pi][:], gcol_ap, rt[r][:], ALU.mult, ALU.mult, [a_ps[pi], a_rt[r], a_cf], f32out[1])
            return
        stt(out0[0:64, :], PS[pi][0:64, :], gcol_ap[0:64, :], rt[r][0:64, :], ALU.mult, ALU.mult,
            [a_ps[pi], a_rt[r], a_cf], out0_at)
        stt(out1[64:128, :], PS[pi][64:128, :], gcol_ap[64:128, :], rt[r][64:128, :], ALU.mult, ALU.mult,
            [a_ps[pi], a_rt[r], a_cf], out1_at)

    def even_prologue(b, l):
        e_ = l // 2
        cosT = r2.view(0, 8192, F32)
        sinT = r2.view(8192, 8192, F32)
        a_cos = r2.at(0, 8192)
        a_sin = r2.at(8192, 8192)
        posi = big.view(0, 8192, I32)
        a_posi = big.at(0, 8192)
        ang = big.view(8192, 8192, F32)
        a_ang = big.at(8192, 8192)
        tmp = big.view(16384, 8192, F32)
        a_tmp = big.at(16384, 8192)
        dma("sp", posi, pos_d[b:b + 1, :].partition_broadcast(128), [], a_posi, "pos")
        cpy(ang, posi, a_posi, a_ang)
        ts_(ang, ang, cfc(C_INV), None, ALU.mult, None, a_ang + [a_cf], a_ang)
        for (dst, a_dst, shift) in ((sinT, a_sin, 0.0), (cosT, a_cos, 0.5 * math.pi)):
            ts_(tmp, ang, shift, 1.0 / TWO_PI, ALU.add, ALU.mult, a_ang, a_tmp)
            ts_(tmp, tmp, MAGIC, None, ALU.add, None, a_tmp, a_tmp)
            ts_(tmp, tmp, MAGIC, None, ALU.subtract, None, a_tmp, a_tmp)
            stt(dst, tmp, -CW1, ang, ALU.mult, ALU.add, a_tmp + a_ang, a_dst)
            stt(dst, tmp, -CW2, dst, ALU.mult, ALU.add, a_tmp + a_dst, a_dst)
            ts_(dst, dst, shift, None, ALU.add, None, a_dst, a_dst)
            ts_(dst, dst, math.pi, -math.pi, ALU.min, ALU.max, a_dst, a_dst)
            act(dst, dst, AF.Sin, a_dst, a_dst)
        FT = big.view(24576, 8192, F32)
        a_FT = big.at(24576, 8192)
        win_l = ewin_d[e_].rearrange("(k p) f -> p k f", p=128)
        dma("pool", wf[:, :, 0:8], win_l[:, :, 1536:1544], [], [a_wf], "wf")
        for tt in range(NTT):
            pi = nxt("sb", 2)
            for kk in range(KD):
                mm(pi, PS[pi][:], wf[:, kk, :], hT[:, kk, tsl(tt)], kk == 0, kk == KD - 1, [a_wf, a_h[kk][tt]])
            s = nxt("sg", 2)
            act(sg[s][:], PS[pi][:], AF.Exp, [a_ps[pi], a_par], [a_sg[s]], scale=-1.0, bias=negfb[:, e_:e_ + 1])
            act(sg[s][:], sg[s][:], AF.Ln, [a_sg[s]], [a_sg[s]], bias=1.0)
            init = 0.0 if tt == 0 else FT[:, tt * 512 - 1:tt * 512]
            k.op("dve", lambda e, tt=tt, s=s, init=init: e.tensor_tensor_scan(
                out=FT[:, tsl(tt)], data0=onesf[:], data1=sg[s][:], initial=init, op0=ALU.mult, op1=ALU.subtract),
                reads=[a_sg[s], a_const] + a_FT, writes=a_FT)
        idf = cf[:, C_ID:C_ID + 128]
        for g in range(4):
            pb = 6 + nxt("nb", 2)
            for c4 in range(4):
                j = g * 4 + c4
                k.op("pe", lambda e, j=j, c4=c4, pb=pb: e.transpose(PS[pb][:, csl(c4)], FT[:, csl(j)], idf),
                     reads=a_FT + [a_cf], writes=[a_ps[pb]])
            ts_(negF[:, g * 4:(g + 1) * 4, :], PS[pb][:].rearrange("p (c n) -> p c n", n=128)[:, :, 0:8], -1.0, None,
                ALU.mult, None, [a_ps[pb]], [a_negF])
        H = big.view(16384, 4096)
        a_H = big.at(16384, 4096)
        ts_(FT[0:8, :], FT[0:8, :], 8.0, None, ALU.mult, None, a_FT, a_FT)
        for part in range(3):
            cpy(H[0:8, :], FT[0:8, :], a_FT, a_H)
            dma("sp", fparts_d[:, part, :], H[0:8, :], a_H, [a_fparts], "fp_w")
            if part < 2:
                tt_(FT[0:8, :], FT[0:8, :], H[0:8, :], ALU.subtract, a_FT + a_H, a_FT)
        return cosT, sinT, a_cos, a_sin

    def pipeline(steps, L=3):
        n = len(steps)
        for i in range(n + L):
            if i < n:
                steps[i][0]()
            if i >= L:
                steps[i - L][1]()

    def norm_heads_out(ab, accv, tt, hh):
        gi = nxt("smg", 4)
        c0 = gi * 16
        recip(sm[:, c0:c0 + 4], accv[:, :, 64], [a_ps[ab]], [a_smg[gi]])
        for ci in range(4):
            ts_(mtok[:, tt * 4 + ci, hh * 64:(hh + 1) * 64], accv[:, ci, 0:64], sm[:, c0 + ci:c0 + ci + 1], None, ALU.mult,
                None, [a_ps[ab], a_smg[gi]], a_mtok)

    def attn_A_unit(p):
        steps = []
        for hh in range(2):
            g = 2 * p + hh
            Qh, a_Qh = (Q0, a_Q0) if hh == 0 else (Q1, a_Q1)
            Kh, a_Kh = (KP0, a_KP0) if hh == 0 else (KP1, a_KP1)
            for tt in range(NTT):
                nj = 4 * (tt + 1)
                ctx = {}
                for j in range(nj):
                    def front(hh=hh, g=g, Qh=Qh, a_Qh=a_Qh, Kh=Kh, a_Kh=a_Kh, tt=tt, j=j, ctx=ctx):
                        if j == 0:
                            ctx["ab"] = 4 + nxt("acc", 2)
                        t0 = max(tt * 512, j * 128)
                        N = (tt + 1) * 512 - t0
                        sbk = SB_RING[nxt("sbr", 4)]
                        mm(sbk, PS[sbk][:, 0:N], Kh[:, csl(j)], Qh[:, t0:t0 + N], True, True, a_Kh + a_Qh[tt])
                        pt = nxt("pt", NPT)
                        act(PT[pt][:, 0:N], PS[sbk][:, 0:N], AF.Exp, [a_ps[sbk], a_negF], [a_PT[pt]], scale=0.125,
                            bias=negF[:, j, g:g + 1])
                        if j * 128 >= tt * 512:
                            tt_(PT[pt][:, 0:128], PT[pt][:, 0:128], tri_bf[:], ALU.mult, [a_PT[pt], a_const], [a_PT[pt]])
                        ctx[j] = pt

                    def back(hh=hh, tt=tt, j=j, ctx=ctx, nj=nj):
                        ab = ctx["ab"]
                        pt = ctx[j]
                        accv = PS[ab][:, 0:260].rearrange("p (c n) -> p c n", n=65)
                        t0 = max(tt * 512, j * 128)
                        for i in range(max(j, 4 * tt), 4 * tt + 4):
                            ci = i - 4 * tt
                            col0 = i * 128 - t0
                            mm(ab, accv[:, ci, :], PT[pt][:, col0:col0 + 128], vtok4[:, j, hh, :], (j == 0 and ci == 0),
                               (j == nj - 1 and ci == 3), [a_PT[pt]] + a_vtok, skip=True)
                        if j == nj - 1:
                            norm_heads_out(ab, accv, tt, hh)
                    steps.append((front, back))
        pipeline(steps)

    def unit_A(l, p):
        e_ = l // 2
        win_l = ewin_d[e_].rearrange("(k p) f -> p k f", p=128)
        wi = load_wi(win_l, [(p * 128, 128), (512 + p * 128, 128), (1024 + p * 128, 128)])
        wo = load_wo(ewout_d[e_, p * 128:(p + 1) * 128, :])
        mset(Q0[64:128, :], 0.0, [x for a in a_Q0 for x in a])
        mset(Q1[0:64, :], 0.0, [x for a in a_Q1 for x in a])
        mset(KP0[64:128, :], 0.0, a_KP0)
        mset(KP1[0:64, :], 0.0, a_KP1)
        mset(KP0[64:67, :], 1.0, a_KP0)
        mset(KP1[0:3, :], 1.0, a_KP1)
        dma("sp", Q0[64:67, :], fparts_d[2 * p, :, :], [a_fparts], [x for a in a_Q0 for x in a], "fp_r0")
        dma("sp", Q1[0:3, :], fparts_d[2 * p + 1, :, :], [a_fparts], [x for a in a_Q1 for x in a], "fp_r1")
        for tt in range(NTT):
            pq = PB_RING[nxt("pb", 4)]
            proj_fm(wi, 0, tt, pq)
            pk_ = PB_RING[nxt("pb", 4)]
            proj_fm(wi, 128, tt, pk_)
            sq_ = ss1([(PS[pq][:], [a_ps[pq]])], bd_bf[:], 512)
            sk_ = ss1([(PS[pk_][:], [a_ps[pk_]])], bd_bf[:], 512)
            rq_ = ss2(sq_, 512, 1.0 / 64)
            rk_ = ss2(sk_, 512, 1.0 / 64)
            headnorm_pair(pq, tt, cfc(C_EG + e_ * 4 + 0), Q0[:, tsl(tt)], a_Q0[tt], Q1[:, tsl(tt)], a_Q1[tt], r=rq_)
            headnorm_pair(pk_, tt, cfc(C_EG + e_ * 4 + 1), KP0[:, tsl(tt)], a_KP0, KP1[:, tsl(tt)], a_KP1, r=rk_)
        proj_v_tok(wi, 256, "A")
        attn_A_unit(p)
        unit_out(wo)

    def unit_B(l, hb, rope):
        e_ = l // 2
        cosT, sinT, a_cos, a_sin = rope
        win_l = ewin_d[e_].rearrange("(k p) f -> p k f", p=128)
        wi = load_wi(win_l, [(1544 + hb * 128, 128), (2056 + hb * 128, 128), (2568 + hb * 128, 128)])
        wo = load_wo(ewout_d[e_, 512 + hb * 128:512 + (hb + 1) * 128, :])
        mset(KP0[64:128, :], 0.0, a_KP0)
        mset(KP1[0:64, :], 0.0, a_KP1)
        y32b = big.view(MOFF, 2048, F32)
        a_y32b = big.at(MOFF, 2048)
        ys = ((y32, a_y32), (y32b, a_y32b))
        for tt in range(NTT):
            pbk = []
            for which in range(2):
                pq = PB_RING[nxt("pb", 4)]
                proj_fm(wi, which * 128, tt, pq)
                pbk.append(pq)
            sums = [ss1([(PS[pbk[w]][:], [a_ps[pbk[w]]])], bd_bf[:], 512) for w in range(2)]
            rr = [ss2(sums[w], 512, 1.0 / 64) for w in range(2)]
            for w in range(2):
                headnorm_pair(pbk[w], tt, cfc(C_EG + e_ * 4 + 2 + w), None, None, None, None, f32out=ys[w], r=rr[w])
            sq2 = []
            for w in range(2):
                s_ = nxt("sq", 2)
                cpy(sqt[s_][:], ys[w][0], ys[w][1], [a_sqt[s_]])
                sq2.append(s_)
            pr = []
            for w in range(2):
                pb = 6 + nxt("nb", 2)
                mm(pb, PS[pb][:], rot_bf[:], sqt[sq2[w]][:], True, True, [a_const, a_sqt[sq2[w]]])
                pr.append(pb)
            gs = []
            for w in range(2):
                g_ = nxt("sg", 2)
                tt_(sg[g_][:], PS[pr[w]][:], sinT[:, tsl(tt)], ALU.mult, [a_ps[pr[w]]] + a_sin, [a_sg[g_]])
                gs.append(g_)
            for w in range(2):
                tt_(ys[w][0], ys[w][0], cosT[:, tsl(tt)], ALU.mult, ys[w][1] + a_cos, ys[w][1])
            tt_(Q0[:, tsl(tt)], ys[0][0], sg[gs[0]][:], ALU.add, ys[0][1] + [a_sg[gs[0]]], a_Q0[tt])
            tt_(KP0[0:64, tsl(tt)], ys[1][0][0:64, :], sg[gs[1]][0:64, :], ALU.add, ys[1][1] + [a_sg[gs[1]]], a_KP0)
            tt_(KP1[64:128, tsl(tt)], ys[1][0][64:128, :], sg[gs[1]][64:128, :], ALU.add, ys[1][1] + [a_sg[gs[1]]], a_KP1)
        proj_v_tok(wi, 256, "B")
        KPs = (KP0, KP1)
        a_KPs = (a_KP0, a_KP1)
        banks = ((4, 5), (0, 1))

        o4 = [big.view(YOFF + i * 512, 512, F32) for i in range(4)]
        a_o4 = [big.at(YOFF + i * 512, 512) for i in range(4)]

        def combine(tt):
            gis = []
            for ci in range(4):
                c0 = (ci % 2) * 129
                b1 = banks[0][ci // 2]
                b2 = banks[1][ci // 2]
                gi = nxt("smg", 4)
                gis.append(gi)
                a_g = a_smg[gi]
                q0 = gi * 16
                recip(sm[:, q0:q0 + 1], PS[b1][:, c0 + 128:c0 + 129], [a_ps[b1]], [a_g])
                recip(sm[:, q0 + 1:q0 + 2], PS[b2][:, c0 + 128:c0 + 129], [a_ps[b2]], [a_g])
                tt_(sm[:, q0 + 2:q0 + 3], sm[:, q0 + 1:q0 + 2], lamt[:, e_:e_ + 1], ALU.mult, [a_g, a_par], [a_g])
                ts_(o4[ci], PS[b1][:, c0:c0 + 128], sm[:, q0:q0 + 1], None, ALU.mult, None, [a_ps[b1], a_g], a_o4[ci])
                stt(o4[ci], PS[b2][:, c0:c0 + 128], sm[:, q0 + 2:q0 + 3], o4[ci], ALU.mult, ALU.add,
                    [a_ps[b2], a_g] + a_o4[ci], a_o4[ci])
            for ci in range(4):
                gi = gis[ci]
                a_g = a_smg[gi]
                q0 = gi * 16
                s_ = nxt("sq", 2)
                act(sqt[s_][:, 0:128], o4[ci], AF.Square, a_o4[ci], [a_sqt[s_], a_g], accum_out=sm[:, q0 + 3:q0 + 4])
                act(sm[:, q0 + 4:q0 + 5], sm[:, q0 + 3:q0 + 4], AF.Ln, [a_g], [a_g], bias=RMS_EPS, scale=1.0 / 128)
                act(sm[:, q0 + 5:q0 + 6], sm[:, q0 + 4:q0 + 5], AF.Exp, [a_g], [a_g], scale=-0.5)
                stt(mtok[:, tt * 4 + ci, :], o4[ci], sm[:, q0 + 5:q0 + 6], subgs[:, e_ * 128:(e_ + 1) * 128], ALU.mult,
                    ALU.mult, a_o4[ci] + [a_g, a_par], a_mtok)

        steps = []
        for tt in range(NTT):
            nj = 4 * (tt + 1)
            for j in range(nj):
                for sub in range(2):
                    ctx = {}

                    def front(tt=tt, j=j, sub=sub, ctx=ctx):
                        t0 = max(tt * 512, j * 128)
                        N = (tt + 1) * 512 - t0
                        sbk = SB_RING[nxt("sbr", 4)]
                        mm(sbk, PS[sbk][:, 0:N], KPs[sub][:, csl(j)], Q0[:, t0:t0 + N], True, True, a_KPs[sub] + a_Q0[tt])
                        pt = nxt("pt", NPT)
                        act(PT[pt][:, 0:N], PS[sbk][:, 0:N], AF.Exp, [a_ps[sbk]], [a_PT[pt]], scale=0.125)
                        if j * 128 >= tt * 512:
                            mset(PT[pt][64:128, 0:64], 0.0, [a_PT[pt]])
                        ctx["pt"] = pt

                    def back(tt=tt, j=j, sub=sub, ctx=ctx, nj=nj):
                        pt = ctx["pt"]
                        t0 = max(tt * 512, j * 128)
                        for i in range(max(j, 4 * tt), 4 * tt + 4):
                            ci = i - 4 * tt
                            col0 = i * 128 - t0
                            bk = banks[sub][ci // 2]
                            mm(bk, PS[bk][:, (ci % 2) * 129:(ci % 2) * 129 + 129], PT[pt][:, col0:col0 + 128],
                               vtok3[:, j, 0:129], (j == 0 and ci % 2 == 0), (j == nj - 1 and ci % 2 == 1),
                               [a_PT[pt]] + a_vtok, skip=True)
                        if j == nj - 1 and sub == 1:
                            combine(tt)
                    steps.append((front, back))
        pipeline(steps)
        unit_out(wo)

    def unit_C(l, j):
        o_ = l // 2
        win_l = owin_d[o_].rearrange("(k p) f -> p k f", p=128)
        wi = load_wi(win_l, [(j * 128, 128), (512 + j * 128, 128), (1024 + j * 128, 128)])
        wo = load_wo(owout_d[o_, j * 128:(j + 1) * 128, :])
        ufull = r2.view(0, 8200, F32)
        a_u = r2.at(0, 8200)
        mset(ufull[:, 0:2], 0.0, a_u)
        cwc = C_CW + (o_ * 4 + j) * 3
        for tt in range(NTT):
            base = 0 if tt % 2 == 0 else 3
            pb_, pc_, ph_ = base, base + 1, base + 2
            proj_fm(wi, 0, tt, pb_)
            proj_fm(wi, 128, tt, pc_)
            proj_fm(wi, 256, tt, ph_)
            s = nxt("sg", 2)
            act(sg[s][:], PS[pc_][:], AF.Copy, [a_ps[pc_]], [a_sg[s]])
            t0 = tt * 512
            tt_(ufull[:, 2 + t0:2 + t0 + 512], sg[s][:], PS[ph_][:], ALU.mult, [a_sg[s], a_ps[ph_]], a_u)
            r = nxt("r", 2)
            ts_(rt[r][:], ufull[:, 2 + t0:2 + t0 + 512], cfc(cwc + 2), None, ALU.mult, None, a_u + [a_cf], [a_rt[r]])
            stt(rt[r][:], ufull[:, 1 + t0:1 + t0 + 512], cfc(cwc + 1), rt[r][:], ALU.mult, ALU.add, a_u + [a_cf, a_rt[r]],
                [a_rt[r]])
            stt(rt[r][:], ufull[:, t0:t0 + 512], cfc(cwc + 0), rt[r][:], ALU.mult, ALU.add, a_u + [a_cf, a_rt[r]], [a_rt[r]])
            tt_(mixT[:, tsl(tt)], PS[pb_][:], rt[r][:], ALU.mult, [a_ps[pb_], a_rt[r]], a_mixT[tt])
        out_proj([(wo, (lambda tt: mixT[:, tsl(tt)]), (lambda tt: a_mixT[tt]))], False)

    def unit_D(l, p):
        o_ = l // 2
        win_l = owin_d[o_].rearrange("(k p) f -> p k f", p=128)
        wi = load_wi(win_l, [(1536 + p * 128, 128), (2048 + p * 128, 128), (2560 + p * 128, 128)])
        wo = load_wo(owout_d[o_, 512 + p * 128:512 + (p + 1) * 128, :])
        BT = [r2.view(8704 + i * 2560, 2560, F32) for i in range(2)]
        a_BT = [r2.at(8704 + i * 2560, 2560) for i in range(2)]
        for hh in range(2):
            h = 2 * p + hh
            dma("sp", BT[hh], ext_d[o_, h, :, :], [], a_BT[hh], "bt%d" % hh)
        mset(BT[0][0:64, 576:640], -30000.0, a_BT[0])
        mset(BT[0][64:128, 0:64], -30000.0, a_BT[0])
        mset(BT[1][0:64, 576:640], -30000.0, a_BT[1])
        mset(BT[1][64:128, 0:64], -30000.0, a_BT[1])
        mset(KP0[64:128, :], 0.0, a_KP0)
        mset(KP1[0:64, :], 0.0, a_KP1)
        for tt in range(NTT):
            pq = PB_RING[nxt("pb", 4)]
            proj_fm(wi, 0, tt, pq)
            pk_ = PB_RING[nxt("pb", 4)]
            proj_fm(wi, 128, tt, pk_)
            sq_ = ss1([(PS[pq][:], [a_ps[pq]])], bd_bf[:], 512)
            sk_ = ss1([(PS[pk_][:], [a_ps[pk_]])], bd_bf[:], 512)
            r = ss2(sq_, 512, 1.0 / 64)
            rk_ = ss2(sk_, 512, 1.0 / 64)
            stt(Q0[:, tsl(tt)], PS[pq][:], cfc(C_OG + o_ * 2 + 0), rt[r][:], ALU.mult, ALU.mult, [a_ps[pq], a_rt[r], a_cf],
                a_Q0[tt])
            headnorm_pair(pk_, tt, cfc(C_OG + o_ * 2 + 1), KP0[:, tsl(tt)], a_KP0, KP1[:, tsl(tt)], a_KP1, r=rk_)
        proj_v_tok(wi, 256, "D")
        KPs = (KP0, KP1)
        a_KPs = (a_KP0, a_KP1)
        steps = []
        for hh in range(2):
            for tt in range(NTT):
                js = list(range(max(0, 4 * tt - 4), 4 * tt + 4))
                ctx = {}
                for j in js:
                    def front(hh=hh, tt=tt, j=j, ctx=ctx, js=js):
                        if j == js[0]:
                            ctx["ab"] = 4 + nxt("acc", 2)
                        i_lo = max(j, 4 * tt)
                        i_hi = min(j + 4, 4 * tt + 3)
                        t0 = i_lo * 128
                        N = (i_hi - i_lo + 1) * 128
                        sbk = SB_RING[nxt("sbr", 4)]
                        mm(sbk, PS[sbk][:, 0:N], KPs[hh][:, csl(j)], Q0[:, t0:t0 + N], True, True, a_KPs[hh] + a_Q0[tt])
                        tl0 = t0 - 128 * j
                        s_ = nxt("sg", 2)
                        stt(sg[s_][:, 0:N], PS[sbk][:, 0:N], 0.125, BT[hh][:, tl0:tl0 + N], ALU.mult, ALU.add,
                            [a_ps[sbk]] + a_BT[hh], [a_sg[s_]])
                        pt = nxt("pt", NPT)
                        act(PT[pt][:, 0:N], sg[s_][:, 0:N], AF.Exp, [a_sg[s_]], [a_PT[pt]])
                        ctx[j] = pt

                    def back(hh=hh, tt=tt, j=j, ctx=ctx, js=js):
                        ab = ctx["ab"]
                        pt = ctx[j]
                        accv = PS[ab][:, 0:260].rearrange("p (c n) -> p c n", n=65)
                        i_lo = max(j, 4 * tt)
                        i_hi = min(j + 4, 4 * tt + 3)
                        for i in range(i_lo, i_hi + 1):
                            ci = i - 4 * tt
                            col0 = (i - i_lo) * 128
                            mm(ab, accv[:, ci, :], PT[pt][:, col0:col0 + 128], vtok4[:, j, hh, :],
                               (j == js[0] and i == i_lo), (j == js[-1] and i == i_hi), [a_PT[pt]] + a_vtok, skip=True)
                        if j == js[-1]:
                            norm_heads_out(ab, accv, tt, hh)
                    steps.append((front, back))
        pipeline(steps)
        unit_out(wo)

    def mixer(b, l):
        rmsnorm_x(l, 1)
        units = range(8) if cfg.units is None else cfg.units
        if l % 2 == 0:
            rope = even_prologue(b, l)
            for u in units:
                if u == list(units)[-1]:
                    arm_after()
                if u < 4:
                    unit_A(l, u)
                else:
                    unit_B(l, u - 4, rope)
        else:
            for u in units:
                if u == list(units)[-1]:
                    arm_after()
                if u < 4:
                    unit_C(l, u)
                else:
                    unit_D(l, u - 4)

    out_toks = []
    for b in range(nseq):
        for kk in range(KD):
            dma("sp", xT[:, kk, :], xT_d[b, kk * 128:(kk + 1) * 128, :], [], a_x[kk], "xin%d" % kk)
        phases = []
        for l in cfg.layers:
            if cfg.do_ffn1:
                phases.append(("ffn1", l, (l, 0)))
            if cfg.do_mixer:
                phases.append(("mixer", l, (l, 1)))
            if cfg.do_xattn:
                phases.append(("xattn", l, (l, 2)))
            if cfg.do_ffn2:
                phases.append(("ffn2", l, (l, 4)))
        hook["prenorm"] = None
        last_l = None
        for pi, (name, l, key) in enumerate(phases):
            if l != last_l:
                k.new_phase()
                last_l = l
            hook["next"] = phases[pi + 1][2] if pi + 1 < len(phases) else None
            hook["after"] = None
            if name == "ffn1":
                ffn(l, 1, f1in_d, f1out_d)
            elif name == "mixer":
                mixer(b, l)
            elif name == "xattn":
                xattn(b, l)
            else:
                ffn(l, 2, f2in_d, f2out_d)
        for kk in range(KD):
            t = dma("sp", outT_d[b, kk * 128:(kk + 1) * 128, :], xT[:, kk, :], a_x[kk], [], "xout%d" % kk)
            out_toks.append(t)
    fin = Op("sp", None, k.phase)
    for t in out_toks:
        fin.waits_d[t[1]] = max(fin.waits_d.get(t[1], 0), t[2])
    k.ops["sp"].append(fin)
    k.emit(nc)
    es.close()
    return nc


def host_consts(inputs):
    cfa = np.zeros((128, NCF), np.float32)
    p = np.arange(128)
    g = inputs["ln_gains"]
    cfa[:, C_GAINS:C_GAINS + 160] = g.reshape(DEPTH, 5, KD, 128).transpose(3, 0, 1, 2).reshape(128, -1)
    eg = inputs["even_qk_gains"]
    cfa[:, C_EG:C_EG + 8] = eg[:, :, p % 64].transpose(2, 0, 1).reshape(128, 8)
    og = inputs["odd_qk_gains"]
    cfa[:, C_OG:C_OG + 4] = og[:, :, p % 64].transpose(2, 0, 1).reshape(128, 4)
    xg = inputs["x_qk_gains"]
    cfa[:, C_XG:C_XG + 16] = xg.reshape(DEPTH, 2, 2, 128).transpose(3, 0, 1, 2).reshape(128, 16)
    cfa[0:8, C_FB:C_FB + 2] = inputs["even_f_bias"].T
    cw = inputs["odd_conv_w"]
    cfa[:, C_CW:C_CW + 24] = cw.reshape(2, 3, 4, 128).transpose(3, 0, 2, 1).reshape(128, 24)
    inv = (np.float32(500000.0) ** (-np.arange(0, 16, 2, dtype=np.float32) / np.float32(16))).astype(np.float32)
    cfa[:, C_INV] = np.where(p % 64 < 16, inv[p % 8], 0.0)
    cfa[:, C_LAM:C_LAM + 512] = inputs["even_lambda"].reshape(1, 512)
    cfa[:, C_SUBG:C_SUBG + 256] = inputs["even_subln_gain"].reshape(1, 256)
    cfa[:, C_ID:C_ID + 128] = np.eye(128, dtype=np.float32)
    cfa[:, C_TRI:C_TRI + 128] = (p[None, :] >= p[:, None]).astype(np.float32)
    cfa[:, C_BD:C_BD + 128] = ((p[None, :] // 64) == (p[:, None] // 64)).astype(np.float32)
    rot = np.zeros((128, 128), np.float32)
    for d in range(128):
        if d % 64 < 8:
            rot[d + 8, d] = -1.0
        elif d % 64 < 16:
            rot[d - 8, d] = 1.0
    cfa[:, C_ROT:C_ROT + 128] = rot
    rb_ = inputs["odd_rel_bias"]
    idx = np.clip(np.arange(640)[None, :] - np.arange(128)[:, None], -128, 128) + 128
    ext = np.ascontiguousarray(rb_[:, :, idx])
    return cfa, ext


def host_inputs(cfg, inputs, core, consts=None):
    nseq = cfg.nseq
    b0 = core * nseq
    m = {}
    m["xT"] = np.ascontiguousarray(np.transpose(inputs["x"][b0:b0 + nseq], (0, 2, 1)))
    m["memT"] = np.ascontiguousarray(np.transpose(inputs["mem"][b0:b0 + nseq], (0, 2, 1)))
    m["pos"] = np.ascontiguousarray(inputs["positions"][b0:b0 + nseq]).astype(np.int32)
    cfa, ext = consts if consts is not None else host_consts(inputs)
    m["cf32"] = cfa
    m["relext"] = ext
    for nm in ("ffn1_w_in", "ffn1_w_out", "ffn2_w_in", "ffn2_w_out", "even_w_in", "even_w_out", "odd_w_in", "odd_w_out",
               "x_w_q", "x_w_kv", "x_w_o"):
        m[nm] = inputs[nm]
    return m


_CACHE = {}


def kernel(**inputs):
    inputs = {k_: np.asarray(v) for k_, v in inputs.items()}
    cfg = Cfg()
    if "nc" not in _CACHE:
        _CACHE["nc"] = build_program(cfg)
    nc = _CACHE["nc"]
    consts = host_consts(inputs)
    in_maps = [host_inputs(cfg, inputs, c, consts) for c in range(NCORES)]
    res = run_bass_kernel_spmd(nc, in_maps, core_ids=list(range(NCORES)))
    outs = [np.transpose(r["outT"], (0, 2, 1)) for r in res.results]
    return np.ascontiguousarray(np.concatenate(outs, axis=0)).astype(np.float32)
```

```python
import math
import numpy as np
import concourse.bass as bass
import concourse.mybir as mybir
from concourse.bass_utils import run_bass_kernel_spmd

F32 = mybir.dt.float32
BF16 = mybir.dt.bfloat16
I32 = mybir.dt.int32
AF = mybir.ActivationFunctionType
ALU = mybir.AluOpType

D_MODEL = 1024
SEQ = 2048
DEPTH = 4
D_FF = 2816
N_MEM = 256
NCORES = 8
RMS_EPS = 1e-6
KD = D_MODEL // 128
NTT = SEQ // 512
NTC = SEQ // 128
NFC = D_FF // 128
EVEN_IN = 3080
ODD_IN = 3072

ENGS = ("pe", "act", "dve", "pool", "sp")
TRUST_INORDER = {"pe": True, "act": False, "dve": False, "pool": False, "sp": True}


class Atom:
    __slots__ = ("lw", "rd", "name")

    def __init__(self, name=""):
        self.lw = None
        self.rd = []
        self.name = name


class Op:
    __slots__ = ("eng", "fn", "waits_e", "waits_d", "signal", "phase", "dma", "snap", "ms")

    def __init__(self, eng, fn, phase):
        self.eng = eng
        self.fn = fn
        self.waits_e = {}
        self.waits_d = {}
        self.signal = False
        self.phase = phase
        self.dma = None
        self.snap = None
        self.ms = None


class K:
    DEBUG_LOG = False
    LAST = None

    def __init__(self):
        self.ops = {e: [] for e in ENGS}
        self.seen = {e: {f: -1 for f in ENGS} for e in ENGS}
        self.seen_d = {e: {} for e in ENGS}
        self.dma_cnt = {}
        self.phase = 0
        self.log = []

    def new_phase(self):
        self.phase += 1

    def _need(self, eng, tok, de, dd):
        if tok is None:
            return
        if tok[0] == "e":
            f, idx = tok[1], tok[2]
            if f == eng and TRUST_INORDER[eng]:
                return
            if idx <= self.seen[eng][f]:
                return
            if idx > de.get(f, -1):
                de[f] = idx
        else:
            key, cnt = tok[1], tok[2]
            if cnt <= self.seen_d[eng].get(key, 0):
                return
            if cnt > dd.get(key, 0):
                dd[key] = cnt

    def op(self, eng, fn, reads=(), writes=(), dma_key=None):
        de, dd = {}, {}
        for a in reads:
            self._need(eng, a.lw, de, dd)
        for a in writes:
            self._need(eng, a.lw, de, dd)
            for t in a.rd:
                self._need(eng, t, de, dd)
        o = Op(eng, fn, self.phase)
        o.waits_e = de
        o.waits_d = dd
        seen = self.seen[eng]
        for f, idx in de.items():
            src = self.ops[f][idx]
            src.signal = True
            if idx > seen[f]:
                seen[f] = idx
            if src.snap is not None:
                for g, v in src.snap.items():
                    if v > seen[g]:
                        seen[g] = v
        sd = self.seen_d[eng]
        for key, cnt in dd.items():
            if cnt > sd.get(key, 0):
                sd[key] = cnt
        idx = len(self.ops[eng])
        self.ops[eng].append(o)
        if K.DEBUG_LOG:
            self.log.append((eng, idx, o, tuple(reads), tuple(writes)))
        seen[eng] = idx if TRUST_INORDER[eng] else seen[eng]
        o.snap = dict(seen)
        if dma_key is not None:
            c = self.dma_cnt.get(dma_key, 0) + 16
            self.dma_cnt[dma_key] = c
            o.dma = dma_key
            tok = ("d", dma_key, c)
        else:
            tok = ("e", eng, idx)
        for a in reads:
            rd = a.rd
            if tok[0] == "e":
                for i, t in enumerate(rd):
                    if t[0] == "e" and t[1] == eng:
                        rd[i] = tok
                        break
                else:
                    rd.append(tok)
            else:
                for i, t in enumerate(rd):
                    if t[0] == "d" and t[1] == tok[1]:
                        rd[i] = tok
                        break
                else:
                    rd.append(tok)
        for a in writes:
            a.lw = tok
            a.rd = []
        return tok

    def emit(self, nc):
        K.LAST = self
        nph = self.phase + 1
        from contextlib import ExitStack
        with ExitStack() as es:
            esems = {}
            for e in ENGS:
                phases = sorted({o.phase for o in self.ops[e] if o.signal})
                for ph in phases:
                    esems[(e, ph)] = es.enter_context(nc.semaphore("s_%s_%d" % (e, ph)))
            dsems = {}
            for key in self.dma_cnt:
                dsems[key] = es.enter_context(nc.semaphore("d_%s" % (key,)))
            for e in ENGS:
                cnt = {}
                for o in self.ops[e]:
                    if o.signal:
                        c = cnt.get(o.phase, 0) + 1
                        cnt[o.phase] = c
                        o.ms = (esems[(e, o.phase)], c)
            block = es.enter_context(nc.Block())
            ops = self.ops

            def run(e, name):
                for o in ops[name]:
                    for f, idx in o.waits_e.items():
                        sem, val = ops[f][idx].ms
                        e.wait_ge(sem, val)
                    for key, cnt_ in o.waits_d.items():
                        e.wait_ge(dsems[key], cnt_)
                    if o.fn is None:
                        continue
                    ins = o.fn(e)
                    if o.dma is not None:
                        ins.then_inc(dsems[o.dma], 16)
                    elif o.signal:
                        ins.then_inc(o.ms[0], 1)

            @block.tensor
            def _(e):
                run(e, "pe")

            @block.scalar
            def _(e):
                run(e, "act")

            @block.vector
            def _(e):
                run(e, "dve")

            @block.gpsimd
            def _(e):
                run(e, "pool")

            @block.sync
            def _(e):
                run(e, "sp")
        print("ops:", {e: len(self.ops[e]) for e in ENGS}, "sems:", len(esems) + len(dsems))


class Cfg:
    def __init__(self, nseq=4, layers=(0, 1, 2, 3), do_ffn1=True, do_mixer=True, do_xattn=True, do_ffn2=True,
                 units=None):
        self.nseq = nseq
        self.layers = tuple(layers)
        self.do_ffn1 = do_ffn1
        self.do_mixer = do_mixer
        self.do_xattn = do_xattn
        self.do_ffn2 = do_ffn2
        self.units = units


C_GAINS = 0
C_EG = C_GAINS + DEPTH * 5 * KD
C_OG = C_EG + 8
C_XG = C_OG + 4
C_FB = C_XG + 16
C_CW = C_FB + 2
C_INV = C_CW + 24
C_LAM = C_INV + 1
C_SUBG = C_LAM + 512
C_ID = C_SUBG + 256
C_TRI = C_ID + 128
C_BD = C_TRI + 128
C_ROT = C_BD + 128
NCF = C_ROT + 128
EXT_LEN = 769
TWO_PI = 2.0 * math.pi
CW1 = 6.28125
CW2 = TWO_PI - CW1
MAGIC = 12582912.0


class Region:
    def __init__(self, t, nbytes):
        self.t = t
        self.atoms = [Atom() for _ in range(nbytes // 512)]

    def at(self, off, nbytes):
        return self.atoms[off // 512:(off + nbytes + 511) // 512]

    def view(self, off, nbytes, dt=None):
        ap = self.t[:, off // 2:(off + nbytes) // 2]
        if dt is not None:
            ap = ap.bitcast(dt)
        return ap


def build_program(cfg):
    nc = bass.Bass("TRN2", target_bir_lowering=False)
    k = K()
    from contextlib import ExitStack
    es = ExitStack()
    nseq = cfg.nseq

    def dram_in(name, shape, dt=F32):
        return nc.dram_tensor(name, list(shape), dt, kind="ExternalInput").ap()

    xT_d = dram_in("xT", [nseq, D_MODEL, SEQ])
    memT_d = dram_in("memT", [nseq, D_MODEL, N_MEM])
    pos_d = dram_in("pos", [nseq, SEQ], I32)
    outT_d = nc.dram_tensor("outT", [nseq, D_MODEL, SEQ], F32, kind="ExternalOutput").ap()
    cf_d = dram_in("cf32", [128, NCF])
    ext_d = dram_in("relext", [2, 8, 128, 640])
    f1in_d = dram_in("ffn1_w_in", [DEPTH, D_MODEL, 2 * D_FF])
    f1out_d = dram_in("ffn1_w_out", [DEPTH, D_FF, D_MODEL])
    f2in_d = dram_in("ffn2_w_in", [DEPTH, D_MODEL, 2 * D_FF])
    f2out_d = dram_in("ffn2_w_out", [DEPTH, D_FF, D_MODEL])
    ewin_d = dram_in("even_w_in", [2, D_MODEL, EVEN_IN])
    ewout_d = dram_in("even_w_out", [2, D_MODEL, D_MODEL])
    owin_d = dram_in("odd_w_in", [2, D_MODEL, ODD_IN])
    owout_d = dram_in("odd_w_out", [2, D_MODEL, D_MODEL])
    xwq_d = dram_in("x_w_q", [DEPTH, D_MODEL, D_MODEL])
    xwkv_d = dram_in("x_w_kv", [DEPTH, D_MODEL, 2 * D_MODEL])
    xwo_d = dram_in("x_w_o", [DEPTH, D_MODEL, D_MODEL])
    fparts_d = nc.dram_tensor("fparts", [8, 3, SEQ], BF16, kind="Internal").ap()
    a_fparts = Atom("fparts")

    def sb(name, shape, dt):
        return es.enter_context(nc.sbuf_tensor(name, list(shape), dt))

    xT = sb("xT_sb", [128, KD, SEQ], F32)
    hT = sb("hT_sb", [128, KD, SEQ], BF16)
    big_t = sb("big_sb", [128, 16384], BF16)
    r2_t = sb("r2_sb", [128, 8192], BF16)
    big = Region(big_t, 32768)
    r2 = Region(r2_t, 16384)
    NWI, NWO = 2, 10
    WI = [sb("wi%d" % i, [128, KD, 392], BF16) for i in range(NWI)]
    WO = [sb("wo%d" % i, [128, 1024], BF16) for i in range(NWO)]
    wf = sb("wf_sb", [128, KD, 128], BF16)
    cf = sb("cf_sb", [128, NCF], F32)
    ones_bf = sb("ones_bf", [128, 128], BF16)
    id_bf = sb("id_bf", [128, 128], BF16)
    tri_bf = sb("tri_bf", [128, 128], BF16)
    bd_bf = sb("bd_bf", [128, 128], BF16)
    rot_bf = sb("rot_bf", [128, 128], BF16)
    onesf = sb("onesf", [128, 512], F32)
    negfb = sb("negfb", [128, 2], F32)
    lamt = sb("lamt", [128, 8], F32)
    subgs = sb("subgs", [128, 256], F32)
    negF = sb("negF", [128, NTC, 8], F32)
    sqt = [sb("sqt%d" % i, [128, 512], BF16) for i in range(2)]
    rt = [sb("rt%d" % i, [128, 512], F32) for i in range(2)]
    sg = [sb("sg%d" % i, [128, 512], F32) for i in range(2)]
    NPT = 6
    PT = [sb("pt%d" % i, [128, 512], BF16) for i in range(NPT)]
    sm = sb("sm_sb", [128, 128], F32)
    PS = [es.enter_context(nc.psum_tensor("ps%d" % i, [128, 512], F32)) for i in range(8)]

    a_x = [[Atom() for tt in range(NTT)] for kk in range(KD)]
    a_h = [[Atom() for tt in range(NTT)] for kk in range(KD)]
    a_WI = [Atom() for _ in range(NWI)]
    a_WO = [Atom() for _ in range(NWO)]
    a_wf = Atom()
    a_ps = [Atom() for i in range(8)]
    K.PS_ATOMS = {a: 'ps%d' % i for i, a in enumerate(a_ps)}
    a_cf = Atom()
    a_const = Atom()
    a_par = Atom()
    a_negF = Atom()
    a_sqt = [Atom() for _ in range(2)]
    a_rt = [Atom() for _ in range(2)]
    a_sg = [Atom() for _ in range(2)]
    a_PT = [Atom() for _ in range(NPT)]
    a_sm = Atom()
    a_smg = [Atom() for _ in range(4)]
    SB_RING = (2, 3, 6, 7)
    PB_RING = (0, 1, 4, 5)
    st = {"wi": 0, "wo": 0, "sq": 0, "r": 0, "sg": 0, "pt": 0, "sb": 0, "acc": 0, "py": 0, "nb": 0, "sbr": 0, "pb": 0, "smg": 0}

    def nxt(key, n):
        v = st[key] % n
        st[key] += 1
        return v

    def mm(pi, out, lhsT, rhs, s, p, reads, skip=False):
        if skip:
            k.op("pe", lambda e: e.matmul(out, lhsT, rhs, start=s, stop=p, skip_group_check=True), reads=reads,
                 writes=[a_ps[pi]])
        else:
            k.op("pe", lambda e: e.matmul(out, lhsT, rhs, start=s, stop=p), reads=reads, writes=[a_ps[pi]])

    def act(out, in_, func, reads, writes, **kw):
        k.op("act", lambda e: e.activation(out, in_, func, **kw), reads=reads, writes=writes)

    def tt_(out, in0, in1, op, reads, writes, eng="dve"):
        k.op(eng, lambda e: e.tensor_tensor(out=out, in0=in0, in1=in1, op=op), reads=reads, writes=writes)

    def ts_(out, in0, s1, s2, op0, op1, reads, writes, eng="dve"):
        if s2 is None:
            k.op(eng, lambda e: e.tensor_scalar(out, in0, s1, None, op0), reads=reads, writes=writes)
        else:
            k.op(eng, lambda e: e.tensor_scalar(out, in0, s1, s2, op0, op1), reads=reads, writes=writes)

    def stt(out, in0, scalar, in1, op0, op1, reads, writes):
        k.op("dve", lambda e: e.scalar_tensor_tensor(out=out, in0=in0, scalar=scalar, in1=in1, op0=op0, op1=op1),
             reads=reads, writes=writes)

    def cpy(out, in_, reads, writes, eng="dve"):
        k.op(eng, lambda e: e.tensor_copy(out, in_), reads=reads, writes=writes)

    def mset(ap, val, writes, eng="dve"):
        k.op(eng, lambda e: e.memset(ap, val), writes=writes)

    def recip(out, in_, reads, writes):
        k.op("dve", lambda e: e.reciprocal(out, in_), reads=reads, writes=writes)

    def dma(eng, out, in_, reads, writes, key):
        return k.op(eng, lambda e: e.dma_start(out=out, in_=in_), reads=reads, writes=writes, dma_key=key)

    def tsl(tt):
        return slice(tt * 512, (tt + 1) * 512)

    def csl(c):
        return slice(c * 128, (c + 1) * 128)

    def cfc(c, n=1):
        return cf[:, c:c + n]

    dma("sp", cf[:], cf_d[:, :], [], [a_cf], "setup")
    mset(ones_bf[:], 1.0, [a_const], eng="pool")
    mset(onesf[:], 1.0, [a_const], eng="pool")
    for (dst, c0) in ((id_bf, C_ID), (tri_bf, C_TRI), (bd_bf, C_BD), (rot_bf, C_ROT)):
        cpy(dst[:], cf[:, c0:c0 + 128], [a_cf], [a_const])
    ts_(negfb[:], cf[:, C_FB:C_FB + 2], -1.0, None, ALU.mult, None, [a_cf], [a_par])
    for e_ in range(2):
        lay = 2 * e_
        lam_init = 0.8 - 0.6 * math.exp(-0.3 * lay)
        base = C_LAM + e_ * 256
        for j in range(2):
            tt_(sg[0][:, 0:64], cf[:, base + (2 * j) * 64:base + (2 * j + 1) * 64],
                cf[:, base + (2 * j + 1) * 64:base + (2 * j + 2) * 64], ALU.mult, [a_cf], [a_sg[0]])
            k.op("dve", lambda e, j=j: e.reduce_sum(sm[:, 104 + j:105 + j], sg[0][:, 0:64], mybir.AxisListType.X),
                 reads=[a_sg[0]], writes=[a_sm])
        act(sm[:, 106:108], sm[:, 104:106], AF.Exp, [a_sm], [a_sm])
        tt_(sm[:, 108:109], sm[:, 106:107], sm[:, 107:108], ALU.subtract, [a_sm], [a_sm])
        ts_(lamt[:, e_:e_ + 1], sm[:, 108:109], lam_init, -1.0, ALU.add, ALU.mult, [a_sm], [a_par])
        ts_(subgs[:, e_ * 128:(e_ + 1) * 128], cf[:, C_SUBG + e_ * 128:C_SUBG + (e_ + 1) * 128], 1.0 - lam_init, None,
            ALU.mult, None, [a_cf], [a_par])
    mset(wf[:], 0.0, [a_wf], eng="pool")

    def gcol(l, i, kk):
        return cfc(C_GAINS + (l * 5 + i) * KD + kk)

    def ss1(srcs, lhs_ones, N):
        pb = 6 + nxt("nb", 2)
        n = len(srcs)
        for i, (ap, ra) in enumerate(srcs):
            s = nxt("sq", 2)
            act(sqt[s][:, 0:N], ap, AF.Square, ra, [a_sqt[s]])
            mm(pb, PS[pb][:, 0:N], lhs_ones, sqt[s][:, 0:N], i == 0, i == n - 1, [a_sqt[s], a_const])
        return pb

    def ss2(pb, N, inv_n):
        r = nxt("r", 2)
        act(rt[r][:, 0:N], PS[pb][:, 0:N], AF.Ln, [a_ps[pb]], [a_rt[r]], bias=RMS_EPS, scale=inv_n)
        act(rt[r][:, 0:N], rt[r][:, 0:N], AF.Exp, [a_rt[r]], [a_rt[r]], scale=-0.5)
        return r

    def sumsq_rstd(srcs, lhs_ones, N, inv_n):
        return ss2(ss1(srcs, lhs_ones, N), N, inv_n)

    hook = {"after": None, "next": None, "prenorm": None}

    def rmsnorm_tt(l, i, tt):
        r = sumsq_rstd([(xT[:, kk, tsl(tt)], [a_x[kk][tt]]) for kk in range(KD)], ones_bf[:], 512, 1.0 / D_MODEL)
        for kk in range(KD):
            stt(hT[:, kk, tsl(tt)], xT[:, kk, tsl(tt)], gcol(l, i, kk), rt[r][:], ALU.mult, ALU.mult,
                [a_x[kk][tt], a_rt[r], a_cf], [a_h[kk][tt]])

    def rmsnorm_x(l, i):
        if hook["prenorm"] == (l, i):
            hook["prenorm"] = None
            return
        for tt in range(NTT):
            rmsnorm_tt(l, i, tt)

    def arm_after():
        key = hook["next"]
        if key is None:
            hook["after"] = None
            return

        def after(tt):
            rmsnorm_tt(key[0], key[1], tt)
            if tt == NTT - 1:
                hook["prenorm"] = key
        hook["after"] = after

    def load_wi(src_l, colgroups):
        wi = nxt("wi", NWI)
        off = 0
        for (c0, n) in colgroups:
            dma("pool", WI[wi][:, :, off:off + n], src_l[:, :, c0:c0 + n], [], [a_WI[wi]], "wi%d" % wi)
            off += n
        return wi

    def load_wo(src_rows):
        wo = nxt("wo", NWO)
        dma("pool", WO[wo][:], src_rows, [], [a_WO[wo]], "wo%d" % wo)
        return wo

    def x_update(o, tt, py, half):
        if half:
            stt(xT[:, o, tsl(tt)], PS[py][:], 0.5, xT[:, o, tsl(tt)], ALU.mult, ALU.add,
                [a_ps[py], a_x[o][tt]], [a_x[o][tt]])
        else:
            tt_(xT[:, o, tsl(tt)], PS[py][:], xT[:, o, tsl(tt)], ALU.add, [a_ps[py], a_x[o][tt]], [a_x[o][tt]])

    def big_chunk(ci, tt):
        off = ci * 4096 + tt * 1024
        return big.view(off, 1024), big.at(off, 1024)

    def out_proj(chunks, half):
        n = len(chunks)
        after = hook["after"]
        hook["after"] = None
        for tt in range(NTT):
            for o in range(KD):
                py = 4 + nxt("py", 2)
                for ci, (wo, apf, atf) in enumerate(chunks):
                    mm(py, PS[py][:], WO[wo][:, csl(o)], apf(tt), ci == 0, ci == n - 1, [a_WO[wo]] + atf(tt))
                x_update(o, tt, py, half)
            if after is not None and tt >= 1:
                after(tt - 1)
        if after is not None:
            after(NTT - 1)

    def pipeline(steps, L=3):
        n = len(steps)
        for i in range(n + L):
            if i < n:
                steps[i][0]()
            if i >= L:
                steps[i - L][1]()

    def ffn(l, which, win_d, wout_d):
        rmsnorm_x(l, 0 if which == 1 else 4)
        groups = [list(range(0, 8)), list(range(8, 15)), list(range(15, 22))]
        win_l = win_d[l].rearrange("(k p) f -> p k f", p=128)
        for grp in groups:
            chunks = []
            for ci, c in enumerate(grp):
                wi = load_wi(win_l, [(c * 128, 128), (D_FF + c * 128, 128)])
                wo = load_wo(wout_d[l, c * 128:(c + 1) * 128, :])
                chunks.append((wo, (lambda tt, ci=ci: big_chunk(ci, tt)[0]), (lambda tt, ci=ci: big_chunk(ci, tt)[1])))
                for tt in range(NTT):
                    pg = (tt % 2) * 2
                    pu = pg + 1
                    for kk in range(KD):
                        mm(pg, PS[pg][:], WI[wi][:, kk, 0:128], hT[:, kk, tsl(tt)], kk == 0, kk == KD - 1,
                           [a_WI[wi], a_h[kk][tt]])
                    for kk in range(KD):
                        mm(pu, PS[pu][:], WI[wi][:, kk, 128:256], hT[:, kk, tsl(tt)], kk == 0, kk == KD - 1,
                           [a_WI[wi], a_h[kk][tt]])
                    s = nxt("sg", 2)
                    act(sg[s][:], PS[pg][:], AF.Silu, [a_ps[pg]], [a_sg[s]])
                    bap, bat = big_chunk(ci, tt)
                    tt_(bap, PS[pu][:], sg[s][:], ALU.mult, [a_ps[pu], a_sg[s]], bat)
            if grp is groups[-1]:
                arm_after()
            out_proj(chunks, True)

    def xattn(b, l):
        memT = r2.view(0, 8192, F32).rearrange("p (k m) -> p k m", m=N_MEM)
        a_mem = r2.at(0, 8192)
        KT = r2.view(0, 4096).rearrange("p (h c m) -> p h c m", h=4, c=2)
        a_KT = r2.at(0, 4096)
        Vt = r2.view(4096, 4096).rearrange("p (mc v) -> p mc v", mc=2)
        a_Vt = r2.at(4096, 4096)
        mnT = r2.view(8192, 4096).rearrange("p (k m) -> p k m", m=N_MEM)
        a_mn = r2.at(8192, 4096)
        qTs = [r2.view(12288 + i * 2048, 2048).rearrange("p (c t) -> p c t", c=2) for i in range(2)]
        a_qT = [r2.at(12288 + i * 2048, 2048) for i in range(2)]
        dma("sp", memT, memT_d[b].rearrange("(k p) m -> p k m", p=128), [], a_mem, "mem")
        r = sumsq_rstd([(memT[:, kk, :], a_mem) for kk in range(KD)], ones_bf[:], N_MEM, 1.0 / D_MODEL)
        for kk in range(KD):
            stt(mnT[:, kk, :], memT[:, kk, :], gcol(l, 3, kk), rt[r][:, 0:N_MEM], ALU.mult, ALU.mult,
                a_mem + [a_rt[r], a_cf], a_mn)
        wkv_l = xwkv_d[l].rearrange("(k p) f -> p k f", p=128)
        for h in range(4):
            wi = load_wi(wkv_l, [(h * 256, 256)])
            pk = nxt("sb", 2)
            for c in range(2):
                for kk in range(KD):
                    mm(pk, PS[pk][:, c * 256:(c + 1) * 256], WI[wi][:, kk, csl(c)], mnT[:, kk, :], kk == 0, kk == KD - 1,
                       [a_WI[wi]] + a_mn)
            r = sumsq_rstd([(PS[pk][:, c * 256:(c + 1) * 256], [a_ps[pk]]) for c in range(2)], ones_bf[:], 256, 1.0 / 256)
            for c in range(2):
                stt(KT[:, h, c, :], PS[pk][:, c * 256:(c + 1) * 256], cfc(C_XG + (l * 2 + 1) * 2 + c), rt[r][:, 0:256],
                    ALU.mult, ALU.mult, [a_ps[pk], a_rt[r], a_cf], a_KT)
        for h in range(4):
            wi = load_wi(wkv_l, [(D_MODEL + h * 256, 256)])
            pk = nxt("sb", 2)
            for mc in range(2):
                for kk in range(KD):
                    mm(pk, PS[pk][:, mc * 256:(mc + 1) * 256], mnT[:, kk, csl(mc)], WI[wi][:, kk, 0:256], kk == 0,
                       kk == KD - 1, [a_WI[wi]] + a_mn)
            for mc in range(2):
                cpy(Vt[:, mc, h * 256:(h + 1) * 256], PS[pk][:, mc * 256:(mc + 1) * 256], [a_ps[pk]], a_Vt)
        rmsnorm_x(l, 2)
        wq_l = xwq_d[l].rearrange("(k p) f -> p k f", p=128)
        scale = 1.0 / 16.0
        xsteps = []
        wis = {}
        for h in range(4):
            for tt in range(NTT):
                ctx = {}

                def front(h=h, tt=tt, ctx=ctx):
                    if tt == 0:
                        wis[h] = load_wi(wq_l, [(h * 256, 256)])
                    wi = wis[h]
                    qi = nxt("acc", 2)
                    qT = qTs[qi]
                    pbase = 2 * nxt("pb", 2)
                    for c in range(2):
                        for kk in range(KD):
                            mm(pbase + c, PS[pbase + c][:], WI[wi][:, kk, csl(c)], hT[:, kk, tsl(tt)], kk == 0, kk == KD - 1,
                               [a_WI[wi], a_h[kk][tt]])
                    r = sumsq_rstd([(PS[pbase + c][:], [a_ps[pbase + c]]) for c in range(2)], ones_bf[:], 512, 1.0 / 256)
                    for c in range(2):
                        stt(qT[:, c, :], PS[pbase + c][:], cfc(C_XG + (l * 2 + 0) * 2 + c), rt[r][:], ALU.mult, ALU.mult,
                            [a_ps[pbase + c], a_rt[r], a_cf], a_qT[qi])
                    ctx["qi"] = qi

                def back(h=h, tt=tt, ctx=ctx):
                    qi = ctx["qi"]
                    qT = qTs[qi]
                    pts = []
                    for mc in range(2):
                        sbk = 4 + mc
                        for c in range(2):
                            mm(sbk, PS[sbk][:], KT[:, h, c, csl(mc)], qT[:, c, :], c == 0, c == 1, a_KT + a_qT[qi])
                        pt = nxt("pt", NPT)
                        act(PT[pt][:], PS[sbk][:], AF.Exp, [a_ps[sbk]], [a_PT[pt]], scale=scale)
                        pts.append(pt)
                    pb = 6 + nxt("nb", 2)
                    for mc in range(2):
                        mm(pb, PS[pb][:], ones_bf[:], PT[pts[mc]][:], mc == 0, mc == 1, [a_const, a_PT[pts[mc]]])
                    r = nxt("r", 2)
                    recip(rt[r][:], PS[pb][:], [a_ps[pb]], [a_rt[r]])
                    for dvc in range(2):
                        po = 4 + dvc
                        for mc in range(2):
                            mm(po, PS[po][:], Vt[:, mc, h * 256 + dvc * 128:h * 256 + (dvc + 1) * 128], PT[pts[mc]][:],
                               mc == 0, mc == 1, a_Vt + [a_PT[pts[mc]]])
                        bap, bat = big_chunk(h * 2 + dvc, tt)
                        tt_(bap, PS[po][:], rt[r][:], ALU.mult, [a_ps[po], a_rt[r]], bat)
                xsteps.append((front, back))
        pipeline(xsteps, L=1)
        chunks = []
        for c in range(8):
            wo = load_wo(xwo_d[l, c * 128:(c + 1) * 128, :])
            chunks.append((wo, (lambda tt, c=c: big_chunk(c, tt)[0]), (lambda tt, c=c: big_chunk(c, tt)[1])))
        arm_after()
        out_proj(chunks, False)

    Q0 = big.view(0, 4096)
    Q1 = big.view(4096, 4096)
    KP0 = big.view(8192, 4096)
    KP1 = big.view(12288, 4096)
    a_Q0 = [big.at(0 + tt * 1024, 1024) for tt in range(NTT)]
    a_Q1 = [big.at(4096 + tt * 1024, 1024) for tt in range(NTT)]
    a_KP0 = big.at(8192, 4096)
    a_KP1 = big.at(12288, 4096)
    VOFF = 16384
    vtok = big.view(VOFF, 16 * 130 * 2)
    vtok4 = vtok.rearrange("p (c h n) -> p c h n", h=2, n=65)
    vtok3 = vtok.rearrange("p (c n) -> p c n", n=130)
    a_vtok = big.at(VOFF, 16 * 130 * 2)
    MOFF = 20992
    mtok = big.view(MOFF, 4096).rearrange("p (c n) -> p c n", n=128)
    a_mtok = big.at(MOFF, 4096)
    XOFF = 25088
    mixT = big.view(XOFF, 4096)
    a_mixT = [big.at(XOFF + tt * 1024, 1024) for tt in range(NTT)]
    YOFF = 29184
    y32 = big.view(YOFF, 2048, F32)
    a_y32 = big.at(YOFF, 2048)
    OOFF = 31232
    o32 = [big.view(OOFF + i * 512, 512, F32) for i in range(2)]
    a_o32 = [big.at(OOFF + i * 512, 512) for i in range(2)]

    def proj_fm(wi, col0, tt, pi):
        for kk in range(KD):
            mm(pi, PS[pi][:], WI[wi][:, kk, col0:col0 + 128], hT[:, kk, tsl(tt)], kk == 0, kk == KD - 1,
               [a_WI[wi], a_h[kk][tt]])

    def proj_v_tok(wi, col0, kind):
        if kind == "B":
            mset(vtok3[:, :, 128:129], 1.0, a_vtok)
        else:
            mset(vtok4[:, :, :, 64:65], 1.0, a_vtok)
        for g in range(4):
            pv = nxt("sb", 2)
            for c4 in range(4):
                tc = g * 4 + c4
                for kk in range(KD):
                    mm(pv, PS[pv][:, csl(c4)], hT[:, kk, csl(tc)], WI[wi][:, kk, col0:col0 + 128], kk == 0, kk == KD - 1,
                       [a_WI[wi], a_h[kk][tc // 4]])
            if kind == "B":
                cpy(vtok3[:, g * 4:(g + 1) * 4, 0:128], PS[pv][:].rearrange("p (c n) -> p c n", n=128), [a_ps[pv]], a_vtok)
            else:
                cpy(vtok4[:, g * 4:(g + 1) * 4, :, 0:64], PS[pv][:].rearrange("p (c h n) -> p c h n", h=2, n=64),
                    [a_ps[pv]], a_vtok)

    def unit_out(wo):
        for tt in range(NTT):
            pb = 6 + nxt("nb", 2)
            psb = PS[pb][:].bitcast(BF16)
            for c4 in range(4):
                k.op("pe", lambda e, c4=c4, tt=tt, psb=psb: e.transpose(psb[:, csl(c4)], mtok[:, tt * 4 + c4, :], id_bf[:]),
                     reads=a_mtok + [a_const], writes=[a_ps[pb]])
            cpy(mixT[:, tsl(tt)], psb[:, 0:512], [a_ps[pb]], a_mixT[tt])
        out_proj([(wo, (lambda tt: mixT[:, tsl(tt)]), (lambda tt: a_mixT[tt]))], False)

    def headnorm_pair(pi, tt, gcol_ap, out0, out0_at, out1, out1_at, f32out=None, r=None):
        if r is None:
            r = sumsq_rstd([(PS[pi][:], [a_ps[pi]])], bd_bf[:], 512, 1.0 / 64)
        if f32out is not None:
            stt(f32out[0], PS[pi][:], gcol_ap, rt[r][:], ALU.mult, ALU.mult, [a_ps[pi], a_rt[r], a_cf], f32out[1])
            return
        stt(out0[0:64, :], PS[pi][0:64, :], gcol_ap[0:64, :], rt[r][0:64, :], ALU.mult, ALU.mult,
            [a_ps[pi], a_rt[r], a_cf], out0_at)
        stt(out1[64:128, :], PS[pi][64:128, :], gcol_ap[64:128, :], rt[r][64:128, :], ALU.mult, ALU.mult,
            [a_ps[pi], a_rt[r], a_cf], out1_at)

    def even_prologue(b, l):
        e_ = l // 2
        cosT = r2.view(0, 8192, F32)
        sinT = r2.view(8192, 8192, F32)
        a_cos = r2.at(0, 8192)
        a_sin = r2.at(8192, 8192)
        posi = big.view(0, 8192, I32)
        a_posi = big.at(0, 8192)
        ang = big.view(8192, 8192, F32)
        a_ang = big.at(8192, 8192)
        tmp = big.view(16384, 8192, F32)
        a_tmp = big.at(16384, 8192)
        dma("sp", posi, pos_d[b:b + 1, :].partition_broadcast(128), [], a_posi, "pos")
        cpy(ang, posi, a_posi, a_ang)
        ts_(ang, ang, cfc(C_INV), None, ALU.mult, None, a_ang + [a_cf], a_ang)
        for (dst, a_dst, shift) in ((sinT, a_sin, 0.0), (cosT, a_cos, 0.5 * math.pi)):
            ts_(tmp, ang, shift, 1.0 / TWO_PI, ALU.add, ALU.mult, a_ang, a_tmp)
            ts_(tmp, tmp, MAGIC, None, ALU.add, None, a_tmp, a_tmp)
            ts_(tmp, tmp, MAGIC, None, ALU.subtract, None, a_tmp, a_tmp)
            stt(dst, tmp, -CW1, ang, ALU.mult, ALU.add, a_tmp + a_ang, a_dst)
            stt(dst, tmp, -CW2, dst, ALU.mult, ALU.add, a_tmp + a_dst, a_dst)
            ts_(dst, dst, shift, None, ALU.add, None, a_dst, a_dst)
            ts_(dst, dst, math.pi, -math.pi, ALU.min, ALU.max, a_dst, a_dst)
            act(dst, dst, AF.Sin, a_dst, a_dst)
        FT = big.view(24576, 8192, F32)
        a_FT = big.at(24576, 8192)
        win_l = ewin_d[e_].rearrange("(k p) f -> p k f", p=128)
        dma("pool", wf[:, :, 0:8], win_l[:, :, 1536:1544], [], [a_wf], "wf")
        for tt in range(NTT):
            pi = nxt("sb", 2)
            for kk in range(KD):
                mm(pi, PS[pi][:], wf[:, kk, :], hT[:, kk, tsl(tt)], kk == 0, kk == KD - 1, [a_wf, a_h[kk][tt]])
            s = nxt("sg", 2)
            act(sg[s][:], PS[pi][:], AF.Exp, [a_ps[pi], a_par], [a_sg[s]], scale=-1.0, bias=negfb[:, e_:e_ + 1])
            act(sg[s][:], sg[s][:], AF.Ln, [a_sg[s]], [a_sg[s]], bias=1.0)
            init = 0.0 if tt == 0 else FT[:, tt * 512 - 1:tt * 512]
            k.op("dve", lambda e, tt=tt, s=s, init=init: e.tensor_tensor_scan(
                out=FT[:, tsl(tt)], data0=onesf[:], data1=sg[s][:], initial=init, op0=ALU.mult, op1=ALU.subtract),
                reads=[a_sg[s], a_const] + a_FT, writes=a_FT)
        idf = cf[:, C_ID:C_ID + 128]
        for g in range(4):
            pb = 6 + nxt("nb", 2)
            for c4 in range(4):
                j = g * 4 + c4
                k.op("pe", lambda e, j=j, c4=c4, pb=pb: e.transpose(PS[pb][:, csl(c4)], FT[:, csl(j)], idf),
                     reads=a_FT + [a_cf], writes=[a_ps[pb]])
            ts_(negF[:, g * 4:(g + 1) * 4, :], PS[pb][:].rearrange("p (c n) -> p c n", n=128)[:, :, 0:8], -1.0, None,
                ALU.mult, None, [a_ps[pb]], [a_negF])
        H = big.view(16384, 4096)
        a_H = big.at(16384, 4096)
        ts_(FT[0:8, :], FT[0:8, :], 8.0, None, ALU.mult, None, a_FT, a_FT)
        for part in range(3):
            cpy(H[0:8, :], FT[0:8, :], a_FT, a_H)
            dma("sp", fparts_d[:, part, :], H[0:8, :], a_H, [a_fparts], "fp_w")
            if part < 2:
                tt_(FT[0:8, :], FT[0:8, :], H[0:8, :], ALU.subtract, a_FT + a_H, a_FT)
        return cosT, sinT, a_cos, a_sin

    def norm_heads_out(ab, accv, tt, hh):
        gi = nxt("smg", 4)
        c0 = gi * 16
        recip(sm[:, c0:c0 + 4], accv[:, :, 64], [a_ps[ab]], [a_smg[gi]])
        for ci in range(4):
            ts_(mtok[:, tt * 4 + ci, hh * 64:(hh + 1) * 64], accv[:, ci, 0:64], sm[:, c0 + ci:c0 + ci + 1], None, ALU.mult,
                None, [a_ps[ab], a_smg[gi]], a_mtok)

    def attn_A_unit(p):
        steps = []
        for hh in range(2):
            g = 2 * p + hh
            Qh, a_Qh = (Q0, a_Q0) if hh == 0 else (Q1, a_Q1)
            Kh, a_Kh = (KP0, a_KP0) if hh == 0 else (KP1, a_KP1)
            for tt in range(NTT):
                nj = 4 * (tt + 1)
                ctx = {}
                for j in range(nj):
                    def front(hh=hh, g=g, Qh=Qh, a_Qh=a_Qh, Kh=Kh, a_Kh=a_Kh, tt=tt, j=j, ctx=ctx):
                        if j == 0:
                            ctx["ab"] = 4 + nxt("acc", 2)
                        t0 = max(tt * 512, j * 128)
                        N = (tt + 1) * 512 - t0
                        sbk = SB_RING[nxt("sbr", 4)]
                        mm(sbk, PS[sbk][:, 0:N], Kh[:, csl(j)], Qh[:, t0:t0 + N], True, True, a_Kh + a_Qh[tt])
                        pt = nxt("pt", NPT)
                        act(PT[pt][:, 0:N], PS[sbk][:, 0:N], AF.Exp, [a_ps[sbk], a_negF], [a_PT[pt]], scale=0.125,
                            bias=negF[:, j, g:g + 1])
                        if j * 128 >= tt * 512:
                            tt_(PT[pt][:, 0:128], PT[pt][:, 0:128], tri_bf[:], ALU.mult, [a_PT[pt], a_const], [a_PT[pt]])
                        ctx[j] = pt

                    def back(hh=hh, tt=tt, j=j, ctx=ctx, nj=nj):
                        ab = ctx["ab"]
                        pt = ctx[j]
                        accv = PS[ab][:, 0:260].rearrange("p (c n) -> p c n", n=65)
                        t0 = max(tt * 512, j * 128)
                        for i in range(max(j, 4 * tt), 4 * tt + 4):
                            ci = i - 4 * tt
                            col0 = i * 128 - t0
                            mm(ab, accv[:, ci, :], PT[pt][:, col0:col0 + 128], vtok4[:, j, hh, :], (j == 0 and ci == 0),
                               (j == nj - 1 and ci == 3), [a_PT[pt]] + a_vtok, skip=True)
                        if j == nj - 1:
                            norm_heads_out(ab, accv, tt, hh)
                    steps.append((front, back))
        pipeline(steps)

    def unit_A(l, p):
        e_ = l // 2
        win_l = ewin_d[e_].rearrange("(k p) f -> p k f", p=128)
        wi = load_wi(win_l, [(p * 128, 128), (512 + p * 128, 128), (1024 + p * 128, 128)])
        wo = load_wo(ewout_d[e_, p * 128:(p + 1) * 128, :])
        mset(Q0[64:128, :], 0.0, [x for a in a_Q0 for x in a])
        mset(Q1[0:64, :], 0.0, [x for a in a_Q1 for x in a])
        mset(KP0[64:128, :], 0.0, a_KP0)
        mset(KP1[0:64, :], 0.0, a_KP1)
        mset(KP0[64:67, :], 1.0, a_KP0)
        mset(KP1[0:3, :], 1.0, a_KP1)
        dma("sp", Q0[64:67, :], fparts_d[2 * p, :, :], [a_fparts], [x for a in a_Q0 for x in a], "fp_r0")
        dma("sp", Q1[0:3, :], fparts_d[2 * p + 1, :, :], [a_fparts], [x for a in a_Q1 for x in a], "fp_r1")
        for tt in range(NTT):
            pq = PB_RING[nxt("pb", 4)]
            proj_fm(wi, 0, tt, pq)
            pk_ = PB_RING[nxt("pb", 4)]
            proj_fm(wi, 128, tt, pk_)
            sq_ = ss1([(PS[pq][:], [a_ps[pq]])], bd_bf[:], 512)
            sk_ = ss1([(PS[pk_][:], [a_ps[pk_]])], bd_bf[:], 512)
            rq_ = ss2(sq_, 512, 1.0 / 64)
            rk_ = ss2(sk_, 512, 1.0 / 64)
            headnorm_pair(pq, tt, cfc(C_EG + e_ * 4 + 0), Q0[:, tsl(tt)], a_Q0[tt], Q1[:, tsl(tt)], a_Q1[tt], r=rq_)
            headnorm_pair(pk_, tt, cfc(C_EG + e_ * 4 + 1), KP0[:, tsl(tt)], a_KP0, KP1[:, tsl(tt)], a_KP1, r=rk_)
        proj_v_tok(wi, 256, "A")
        attn_A_unit(p)
        unit_out(wo)

    def unit_B(l, hb, rope):
        e_ = l // 2
        cosT, sinT, a_cos, a_sin = rope
        win_l = ewin_d[e_].rearrange("(k p) f -> p k f", p=128)
        wi = load_wi(win_l, [(1544 + hb * 128, 128), (2056 + hb * 128, 128), (2568 + hb * 128, 128)])
        wo = load_wo(ewout_d[e_, 512 + hb * 128:512 + (hb + 1) * 128, :])
        mset(KP0[64:128, :], 0.0, a_KP0)
        mset(KP1[0:64, :], 0.0, a_KP1)
        y32b = big.view(MOFF, 2048, F32)
        a_y32b = big.at(MOFF, 2048)
        ys = ((y32, a_y32), (y32b, a_y32b))
        for tt in range(NTT):
            pbk = []
            for which in range(2):
                pq = PB_RING[nxt("pb", 4)]
                proj_fm(wi, which * 128, tt, pq)
                pbk.append(pq)
            sums = [ss1([(PS[pbk[w]][:], [a_ps[pbk[w]]])], bd_bf[:], 512) for w in range(2)]
            rr = [ss2(sums[w], 512, 1.0 / 64) for w in range(2)]
            for w in range(2):
                headnorm_pair(pbk[w], tt, cfc(C_EG + e_ * 4 + 2 + w), None, None, None, None, f32out=ys[w], r=rr[w])
            sq2 = []
            for w in range(2):
                s_ = nxt("sq", 2)
                cpy(sqt[s_][:], ys[w][0], ys[w][1], [a_sqt[s_]])
                sq2.append(s_)
            pr = []
            for w in range(2):
                pb = 6 + nxt("nb", 2)
                mm(pb, PS[pb][:], rot_bf[:], sqt[sq2[w]][:], True, True, [a_const, a_sqt[sq2[w]]])
                pr.append(pb)
            gs = []
            for w in range(2):
                g_ = nxt("sg", 2)
                tt_(sg[g_][:], PS[pr[w]][:], sinT[:, tsl(tt)], ALU.mult, [a_ps[pr[w]]] + a_sin, [a_sg[g_]])
                gs.append(g_)
            for w in range(2):
                tt_(ys[w][0], ys[w][0], cosT[:, tsl(tt)], ALU.mult, ys[w][1] + a_cos, ys[w][1])
            tt_(Q0[:, tsl(tt)], ys[0][0], sg[gs[0]][:], ALU.add, ys[0][1] + [a_sg[gs[0]]], a_Q0[tt])
            tt_(KP0[0:64, tsl(tt)], ys[1][0][0:64, :], sg[gs[1]][0:64, :], ALU.add, ys[1][1] + [a_sg[gs[1]]], a_KP0)
            tt_(KP1[64:128, tsl(tt)], ys[1][0][64:128, :], sg[gs[1]][64:128, :], ALU.add, ys[1][1] + [a_sg[gs[1]]], a_KP1)
        proj_v_tok(wi, 256, "B")
        KPs = (KP0, KP1)
        a_KPs = (a_KP0, a_KP1)
        banks = ((4, 5), (0, 1))

        o4 = [big.view(YOFF + i * 512, 512, F32) for i in range(4)]
        a_o4 = [big.at(YOFF + i * 512, 512) for i in range(4)]

        def combine(tt):
            gis = []
            for ci in range(4):
                c0 = (ci % 2) * 129
                b1 = banks[0][ci // 2]
                b2 = banks[1][ci // 2]
                gi = nxt("smg", 4)
                gis.append(gi)
                a_g = a_smg[gi]
                q0 = gi * 16
                recip(sm[:, q0:q0 + 1], PS[b1][:, c0 + 128:c0 + 129], [a_ps[b1]], [a_g])
                recip(sm[:, q0 + 1:q0 + 2], PS[b2][:, c0 + 128:c0 + 129], [a_ps[b2]], [a_g])
                tt_(sm[:, q0 + 2:q0 + 3], sm[:, q0 + 1:q0 + 2], lamt[:, e_:e_ + 1], ALU.mult, [a_g, a_par], [a_g])
                ts_(o4[ci], PS[b1][:, c0:c0 + 128], sm[:, q0:q0 + 1], None, ALU.mult, None, [a_ps[b1], a_g], a_o4[ci])
                stt(o4[ci], PS[b2][:, c0:c0 + 128], sm[:, q0 + 2:q0 + 3], o4[ci], ALU.mult, ALU.add,
                    [a_ps[b2], a_g] + a_o4[ci], a_o4[ci])
            for ci in range(4):
                gi = gis[ci]
                a_g = a_smg[gi]
                q0 = gi * 16
                s_ = nxt("sq", 2)
                act(sqt[s_][:, 0:128], o4[ci], AF.Square, a_o4[ci], [a_sqt[s_], a_g], accum_out=sm[:, q0 + 3:q0 + 4])
                act(sm[:, q0 + 4:q0 + 5], sm[:, q0 + 3:q0 + 4], AF.Ln, [a_g], [a_g], bias=RMS_EPS, scale=1.0 / 128)
                act(sm[:, q0 + 5:q0 + 6], sm[:, q0 + 4:q0 + 5], AF.Exp, [a_g], [a_g], scale=-0.5)
                stt(mtok[:, tt * 4 + ci, :], o4[ci], sm[:, q0 + 5:q0 + 6], subgs[:, e_ * 128:(e_ + 1) * 128], ALU.mult,
                    ALU.mult, a_o4[ci] + [a_g, a_par], a_mtok)

        steps = []
        for tt in range(NTT):
            nj = 4 * (tt + 1)
            for j in range(nj):
                for sub in range(2):
                    ctx = {}

                    def front(tt=tt, j=j, sub=sub, ctx=ctx):
                        t0 = max(tt * 512, j * 128)
                        N = (tt + 1) * 512 - t0
                        sbk = SB_RING[nxt("sbr", 4)]
                        mm(sbk, PS[sbk][:, 0:N], KPs[sub][:, csl(j)], Q0[:, t0:t0 + N], True, True, a_KPs[sub] + a_Q0[tt])
                        pt = nxt("pt", NPT)
                        act(PT[pt][:, 0:N], PS[sbk][:, 0:N], AF.Exp, [a_ps[sbk]], [a_PT[pt]], scale=0.125)
                        if j * 128 >= tt * 512:
                            mset(PT[pt][64:128, 0:64], 0.0, [a_PT[pt]])
                        ctx["pt"] = pt

                    def back(tt=tt, j=j, sub=sub, ctx=ctx, nj=nj):
                        pt = ctx["pt"]
                        t0 = max(tt * 512, j * 128)
                        for i in range(max(j, 4 * tt), 4 * tt + 4):
                            ci = i - 4 * tt
                            col0 = i * 128 - t0
                            bk = banks[sub][ci // 2]
                            mm(bk, PS[bk][:, (ci % 2) * 129:(ci % 2) * 129 + 129], PT[pt][:, col0:col0 + 128],
                               vtok3[:, j, 0:129], (j == 0 and ci % 2 == 0), (j == nj - 1 and ci % 2 == 1),
                               [a_PT[pt]] + a_vtok, skip=True)
                        if j == nj - 1 and sub == 1:
                            combine(tt)
                    steps.append((front, back))
        pipeline(steps)
        unit_out(wo)

    def unit_C(l, j):
        o_ = l // 2
        win_l = owin_d[o_].rearrange("(k p) f -> p k f", p=128)
        wi = load_wi(win_l, [(j * 128, 128), (512 + j * 128, 128), (1024 + j * 128, 128)])
        wo = load_wo(owout_d[o_, j * 128:(j + 1) * 128, :])
        ufull = r2.view(0, 8200, F32)
        a_u = r2.at(0, 8200)
        mset(ufull[:, 0:2], 0.0, a_u)
        cwc = C_CW + (o_ * 4 + j) * 3
        for tt in range(NTT):
            base = 0 if tt % 2 == 0 else 3
            pb_, pc_, ph_ = base, base + 1, base + 2
            proj_fm(wi, 0, tt, pb_)
            proj_fm(wi, 128, tt, pc_)
            proj_fm(wi, 256, tt, ph_)
            s = nxt("sg", 2)
            act(sg[s][:], PS[pc_][:], AF.Copy, [a_ps[pc_]], [a_sg[s]])
            t0 = tt * 512
            tt_(ufull[:, 2 + t0:2 + t0 + 512], sg[s][:], PS[ph_][:], ALU.mult, [a_sg[s], a_ps[ph_]], a_u)
            r = nxt("r", 2)
            ts_(rt[r][:], ufull[:, 2 + t0:2 + t0 + 512], cfc(cwc + 2), None, ALU.mult, None, a_u + [a_cf], [a_rt[r]])
            stt(rt[r][:], ufull[:, 1 + t0:1 + t0 + 512], cfc(cwc + 1), rt[r][:], ALU.mult, ALU.add, a_u + [a_cf, a_rt[r]],
                [a_rt[r]])
            stt(rt[r][:], ufull[:, t0:t0 + 512], cfc(cwc + 0), rt[r][:], ALU.mult, ALU.add, a_u + [a_cf, a_rt[r]], [a_rt[r]])
            tt_(mixT[:, tsl(tt)], PS[pb_][:], rt[r][:], ALU.mult, [a_ps[pb_], a_rt[r]], a_mixT[tt])
        out_proj([(wo, (lambda tt: mixT[:, tsl(tt)]), (lambda tt: a_mixT[tt]))], False)

    def unit_D(l, p):
        o_ = l // 2
        win_l = owin_d[o_].rearrange("(k p) f -> p k f", p=128)
        wi = load_wi(win_l, [(1536 + p * 128, 128), (2048 + p * 128, 128), (2560 + p * 128, 128)])
        wo = load_wo(owout_d[o_, 512 + p * 128:512 + (p + 1) * 128, :])
        BT = [r2.view(8704 + i * 2560, 2560, F32) for i in range(2)]
        a_BT = [r2.at(8704 + i * 2560, 2560) for i in range(2)]
        for hh in range(2):
            h = 2 * p + hh
            dma("sp", BT[hh], ext_d[o_, h, :, :], [], a_BT[hh], "bt%d" % hh)
        mset(BT[0][0:64, 576:640], -30000.0, a_BT[0])
        mset(BT[0][64:128, 0:64], -30000.0, a_BT[0])
        mset(BT[1][0:64, 576:640], -30000.0, a_BT[1])
        mset(BT[1][64:128, 0:64], -30000.0, a_BT[1])
        mset(KP0[64:128, :], 0.0, a_KP0)
        mset(KP1[0:64, :], 0.0, a_KP1)
        for tt in range(NTT):
            pq = PB_RING[nxt("pb", 4)]
            proj_fm(wi, 0, tt, pq)
            pk_ = PB_RING[nxt("pb", 4)]
            proj_fm(wi, 128, tt, pk_)
            sq_ = ss1([(PS[pq][:], [a_ps[pq]])], bd_bf[:], 512)
            sk_ = ss1([(PS[pk_][:], [a_ps[pk_]])], bd_bf[:], 512)
            r = ss2(sq_, 512, 1.0 / 64)
            rk_ = ss2(sk_, 512, 1.0 / 64)
            stt(Q0[:, tsl(tt)], PS[pq][:], cfc(C_OG + o_ * 2 + 0), rt[r][:], ALU.mult, ALU.mult, [a_ps[pq], a_rt[r], a_cf],
                a_Q0[tt])
            headnorm_pair(pk_, tt, cfc(C_OG + o_ * 2 + 1), KP0[:, tsl(tt)], a_KP0, KP1[:, tsl(tt)], a_KP1, r=rk_)
        proj_v_tok(wi, 256, "D")
        KPs = (KP0, KP1)
        a_KPs = (a_KP0, a_KP1)
        steps = []
        for hh in range(2):
            for tt in range(NTT):
                js = list(range(max(0, 4 * tt - 4), 4 * tt + 4))
                ctx = {}
                for j in js:
                    def front(hh=hh, tt=tt, j=j, ctx=ctx, js=js):
                        if j == js[0]:
                            ctx["ab"] = 4 + nxt("acc", 2)
                        i_lo = max(j, 4 * tt)
                        i_hi = min(j + 4, 4 * tt + 3)
                        t0 = i_lo * 128
                        N = (i_hi - i_lo + 1) * 128
                        sbk = SB_RING[nxt("sbr", 4)]
                        mm(sbk, PS[sbk][:, 0:N], KPs[hh][:, csl(j)], Q0[:, t0:t0 + N], True, True, a_KPs[hh] + a_Q0[tt])
                        tl0 = t0 - 128 * j
                        s_ = nxt("sg", 2)
                        stt(sg[s_][:, 0:N], PS[sbk][:, 0:N], 0.125, BT[hh][:, tl0:tl0 + N], ALU.mult, ALU.add,
                            [a_ps[sbk]] + a_BT[hh], [a_sg[s_]])
                        pt = nxt("pt", NPT)
                        act(PT[pt][:, 0:N], sg[s_][:, 0:N], AF.Exp, [a_sg[s_]], [a_PT[pt]])
                        ctx[j] = pt

                    def back(hh=hh, tt=tt, j=j, ctx=ctx, js=js):
                        ab = ctx["ab"]
                        pt = ctx[j]
                        accv = PS[ab][:, 0:260].rearrange("p (c n) -> p c n", n=65)
                        i_lo = max(j, 4 * tt)
                        i_hi = min(j + 4, 4 * tt + 3)
                        for i in range(i_lo, i_hi + 1):
                            ci = i - 4 * tt
                            col0 = (i - i_lo) * 128
                            mm(ab, accv[:, ci, :], PT[pt][:, col0:col0 + 128], vtok4[:, j, hh, :],
                               (j == js[0] and i == i_lo), (j == js[-1] and i == i_hi), [a_PT[pt]] + a_vtok, skip=True)
                        if j == js[-1]:
                            norm_heads_out(ab, accv, tt, hh)
                    steps.append((front, back))
        pipeline(steps)
        unit_out(wo)

    def mixer(b, l):
        rmsnorm_x(l, 1)
        units = range(8) if cfg.units is None else cfg.units
        if l % 2 == 0:
            rope = even_prologue(b, l)
            for u in units:
                if u == list(units)[-1]:
                    arm_after()
                if u < 4:
                    unit_A(l, u)
                else:
                    unit_B(l, u - 4, rope)
        else:
            for u in units:
                if u == list(units)[-1]:
                    arm_after()
                if u < 4:
                    unit_C(l, u)
                else:
                    unit_D(l, u - 4)

    out_toks = []
    for b in range(nseq):
        for kk in range(KD):
            dma("sp", xT[:, kk, :], xT_d[b, kk * 128:(kk + 1) * 128, :], [], a_x[kk], "xin%d" % kk)
        phases = []
        for l in cfg.layers:
            if cfg.do_ffn1:
                phases.append(("ffn1", l, (l, 0)))
            if cfg.do_mixer:
                phases.append(("mixer", l, (l, 1)))
            if cfg.do_xattn:
                phases.append(("xattn", l, (l, 2)))
            if cfg.do_ffn2:
                phases.append(("ffn2", l, (l, 4)))
        hook["prenorm"] = None
        last_l = None
        for pi, (name, l, key) in enumerate(phases):
            if l != last_l:
                k.new_phase()
                last_l = l
            hook["next"] = phases[pi + 1][2] if pi + 1 < len(phases) else None
            hook["after"] = None
            if name == "ffn1":
                ffn(l, 1, f1in_d, f1out_d)
            elif name == "mixer":
                mixer(b, l)
            elif name == "xattn":
                xattn(b, l)
            else:
                ffn(l, 2, f2in_d, f2out_d)
        for kk in range(KD):
            t = dma("sp", outT_d[b, kk * 128:(kk + 1) * 128, :], xT[:, kk, :], a_x[kk], [], "xout%d" % kk)
            out_toks.append(t)
    fin = Op("sp", None, k.phase)
    for t in out_toks:
        fin.waits_d[t[1]] = max(fin.waits_d.get(t[1], 0), t[2])
    k.ops["sp"].append(fin)
    k.emit(nc)
    es.close()
    return nc


def host_consts(inputs):
    cfa = np.zeros((128, NCF), np.float32)
    p = np.arange(128)
    g = inputs["ln_gains"]
    cfa[:, C_GAINS:C_GAINS + 160] = g.reshape(DEPTH, 5, KD, 128).transpose(3, 0, 1, 2).reshape(128, -1)
    eg = inputs["even_qk_gains"]
    cfa[:, C_EG:C_EG + 8] = eg[:, :, p % 64].transpose(2, 0, 1).reshape(128, 8)
    og = inputs["odd_qk_gains"]
    cfa[:, C_OG:C_OG + 4] = og[:, :, p % 64].transpose(2, 0, 1).reshape(128, 4)
    xg = inputs["x_qk_gains"]
    cfa[:, C_XG:C_XG + 16] = xg.reshape(DEPTH, 2, 2, 128).transpose(3, 0, 1, 2).reshape(128, 16)
    cfa[0:8, C_FB:C_FB + 2] = inputs["even_f_bias"].T
    cw = inputs["odd_conv_w"]
    cfa[:, C_CW:C_CW + 24] = cw.reshape(2, 3, 4, 128).transpose(3, 0, 2, 1).reshape(128, 24)
    inv = (np.float32(500000.0) ** (-np.arange(0, 16, 2, dtype=np.float32) / np.float32(16))).astype(np.float32)
    cfa[:, C_INV] = np.where(p % 64 < 16, inv[p % 8], 0.0)
    cfa[:, C_LAM:C_LAM + 512] = inputs["even_lambda"].reshape(1, 512)
    cfa[:, C_SUBG:C_SUBG + 256] = inputs["even_subln_gain"].reshape(1, 256)
    cfa[:, C_ID:C_ID + 128] = np.eye(128, dtype=np.float32)
    cfa[:, C_TRI:C_TRI + 128] = (p[None, :] >= p[:, None]).astype(np.float32)
    cfa[:, C_BD:C_BD + 128] = ((p[None, :] // 64) == (p[:, None] // 64)).astype(np.float32)
    rot = np.zeros((128, 128), np.float32)
    for d in range(128):
        if d % 64 < 8:
            rot[d + 8, d] = -1.0
        elif d % 64 < 16:
            rot[d - 8, d] = 1.0
    cfa[:, C_ROT:C_ROT + 128] = rot
    rb_ = inputs["odd_rel_bias"]
    idx = np.clip(np.arange(640)[None, :] - np.arange(128)[:, None], -128, 128) + 128
    ext = np.ascontiguousarray(rb_[:, :, idx])
    return cfa, ext


def host_inputs(cfg, inputs, core, consts=None):
    nseq = cfg.nseq
    b0 = core * nseq
    m = {}
    m["xT"] = np.ascontiguousarray(np.transpose(inputs["x"][b0:b0 + nseq], (0, 2, 1)))
    m["memT"] = np.ascontiguousarray(np.transpose(inputs["mem"][b0:b0 + nseq], (0, 2, 1)))
    m["pos"] = np.ascontiguousarray(inputs["positions"][b0:b0 + nseq]).astype(np.int32)
    cfa, ext = consts if consts is not None else host_consts(inputs)
    m["cf32"] = cfa
    m["relext"] = ext
    for nm in ("ffn1_w_in", "ffn1_w_out", "ffn2_w_in", "ffn2_w_out", "even_w_in", "even_w_out", "odd_w_in", "odd_w_out",
               "x_w_q", "x_w_kv", "x_w_o"):
        m[nm] = inputs[nm]
    return m


_CACHE = {}


def kernel(**inputs):
    inputs = {k_: np.asarray(v) for k_, v in inputs.items()}
    cfg = Cfg()
    if "nc" not in _CACHE:
        _CACHE["nc"] = build_program(cfg)
    nc = _CACHE["nc"]
    consts = host_consts(inputs)
    in_maps = [host_inputs(cfg, inputs, c, consts) for c in range(NCORES)]
    res = run_bass_kernel_spmd(nc, in_maps, core_ids=list(range(NCORES)))
    outs = [np.transpose(r["outT"], (0, 2, 1)) for r in res.results]
    return np.ascontiguousarray(np.concatenate(outs, axis=0)).astype(np.float32)
```
